# Optimizing a Trainium2 kernel written in Bass

```python
import jax, jax.numpy as jnp
from jax import lax
import numpy as np

D_MODEL = 1024
BATCH = 1
SEQ = 16384
DEPTH = 4
DEC_BATCH = 2
DEC_SEQ = 8192
PAST_LEN = 128

N_MIXERS = 2
N_ATTN_LAYERS = (DEPTH + 1) // 2
N_FOURIER_LAYERS = DEPTH // 2
HEAD_DIM = 64
N_Q_HEADS = D_MODEL // HEAD_DIM
N_KV_HEADS = 4
GQA_GROUP = N_Q_HEADS // N_KV_HEADS
QKV_DIM = (N_Q_HEADS + 2 * N_KV_HEADS) * HEAD_DIM
WINDOW = 128
BLOCK = 128
ROPE_THETA = 10000.0
N_FOURIER_GROUPS = 4
FOURIER_GROUP_DIM = D_MODEL // N_FOURIER_GROUPS
D_FF = -(-8 * D_MODEL // (3 * 256)) * 256
EPS = 1e-6
NEG_INF = -1e30

kernel_name = "hybrid_swa_fnet_encoder"


def rms_norm(x, g):
    xf = x.astype(jnp.float32)
    y = xf * lax.rsqrt(jnp.mean(xf * xf, axis=-1, keepdims=True) + EPS)
    return (y * g.astype(jnp.float32)).astype(x.dtype)


def rope(x, pos):
    half = HEAD_DIM // 2
    inv_freq = ROPE_THETA ** (-jnp.arange(half, dtype=jnp.float32) / half)
    ang = pos.astype(jnp.float32)[:, None] * inv_freq[None, :]
    cos = jnp.cos(ang)[None, :, None, :]
    sin = jnp.sin(ang)[None, :, None, :]
    xf = x.astype(jnp.float32)
    x1, x2 = xf[..., :half], xf[..., half:]
    return jnp.concatenate([x1 * cos - x2 * sin, x2 * cos + x1 * sin], axis=-1).astype(x.dtype)


def band_blocks(t):
    b, s, h, d = t.shape
    nb = s // BLOCK
    tp = jnp.pad(t, ((0, 0), (BLOCK, BLOCK), (0, 0), (0, 0))).reshape(b, nb + 2, BLOCK, h, d)
    return jnp.concatenate([tp[:, :-2], tp[:, 1:-1], tp[:, 2:]], axis=2)


def windowed_attention(xn, w_qkv, q_norm_g, k_norm_g, sinks, w_o):
    b, s, _ = xn.shape
    nb = s // BLOCK
    qkv = xn @ w_qkv
    q, k, v = jnp.split(qkv, [N_Q_HEADS * HEAD_DIM, (N_Q_HEADS + N_KV_HEADS) * HEAD_DIM], axis=-1)
    q = q.reshape(b, s, N_Q_HEADS, HEAD_DIM)
    k = k.reshape(b, s, N_KV_HEADS, HEAD_DIM)
    v = v.reshape(b, s, N_KV_HEADS, HEAD_DIM)
    q = rms_norm(q, q_norm_g)
    k = rms_norm(k, k_norm_g)
    pos = jnp.arange(s)
    q = rope(q, pos)
    k = rope(k, pos)
    qb = q.reshape(b, nb, BLOCK, N_KV_HEADS, GQA_GROUP, HEAD_DIM).astype(jnp.float32)
    kw = band_blocks(k).astype(jnp.float32)
    vw = band_blocks(v).astype(jnp.float32)
    scores = jnp.einsum('bnqhgd,bnkhd->bnhgqk', qb, kw) * (HEAD_DIM ** -0.5)
    qi = jnp.arange(BLOCK)[:, None]
    kj = jnp.arange(3 * BLOCK)[None, :]
    rel = kj - BLOCK - qi
    kabs = jnp.arange(nb)[:, None, None] * BLOCK - BLOCK + kj[None]
    mask = (jnp.abs(rel) <= WINDOW)[None] & (kabs >= 0) & (kabs < s)
    scores = jnp.where(mask[None, :, None, None], scores, NEG_INF)
    sink = sinks.astype(jnp.float32).reshape(N_KV_HEADS, GQA_GROUP)[None, None, :, :, None, None]
    m = jnp.maximum(jnp.max(scores, axis=-1, keepdims=True), sink)
    p = jnp.exp(scores - m)
    denom = jnp.sum(p, axis=-1, keepdims=True) + jnp.exp(sink - m)
    out = jnp.einsum('bnhgqk,bnkhd->bnqhgd', p / denom, vw)
    out = out.reshape(b, s, N_Q_HEADS * HEAD_DIM).astype(xn.dtype)
    return out @ w_o


def fourier_mix(xn, w_out):
    b, s, d = xn.shape
    xg = xn.astype(jnp.float32).reshape(b, s, N_FOURIER_GROUPS, FOURIER_GROUP_DIM)
    f = jnp.fft.fft2(xg, axes=(1, 3), norm="ortho").real
    return f.reshape(b, s, d).astype(xn.dtype) @ w_out


def swiglu(xn, w_gate_up, w_down):
    g, u = jnp.split(xn @ w_gate_up, 2, axis=-1)
    return (jax.nn.silu(g) * u) @ w_down


def trunk(x, attn_norm_g, w_qkv, q_norm_g, k_norm_g, attn_sinks, w_o_attn,
          fourier_norm_g, w_fourier_out, ffn_norm_g, w_gate_up, w_down):
    for i in range(DEPTH):
        j = i // N_MIXERS
        if i % N_MIXERS == 0:
            x = x + windowed_attention(rms_norm(x, attn_norm_g[j]), w_qkv[j], q_norm_g[j],
                                       k_norm_g[j], attn_sinks[j], w_o_attn[j])
        else:
            x = x + fourier_mix(rms_norm(x, fourier_norm_g[j]), w_fourier_out[j])
        x = x + swiglu(rms_norm(x, ffn_norm_g[i]), w_gate_up[i], w_down[i])
    return x


def setup_inputs(seed: int = 0) -> dict:
    key = jax.random.key(seed)
    ks = jax.random.split(key, 14)
    f32 = jnp.float32
    nrm = lambda k, shape, scale: jax.random.normal(k, shape, f32) * scale
    return {
        "x_prompt": nrm(ks[0], (BATCH, SEQ, D_MODEL), 1.0),
        "x_sample": nrm(ks[1], (DEC_BATCH, DEC_SEQ, D_MODEL), 1.0),
        "attn_norm_g": 1.0 + nrm(ks[2], (N_ATTN_LAYERS, D_MODEL), 0.02),
        "w_qkv": nrm(ks[3], (N_ATTN_LAYERS, D_MODEL, QKV_DIM), D_MODEL ** -0.5),
        "q_norm_g": 1.0 + nrm(ks[4], (N_ATTN_LAYERS, HEAD_DIM), 0.02),
        "k_norm_g": 1.0 + nrm(ks[5], (N_ATTN_LAYERS, HEAD_DIM), 0.02),
        "attn_sinks": nrm(ks[6], (N_ATTN_LAYERS, N_Q_HEADS), 1.0),
        "w_o_attn": nrm(ks[7], (N_ATTN_LAYERS, N_Q_HEADS * HEAD_DIM, D_MODEL), (N_Q_HEADS * HEAD_DIM) ** -0.5),
        "fourier_norm_g": 1.0 + nrm(ks[8], (N_FOURIER_LAYERS, D_MODEL), 0.02),
        "w_fourier_out": nrm(ks[9], (N_FOURIER_LAYERS, D_MODEL, D_MODEL), D_MODEL ** -0.5),
        "ffn_norm_g": 1.0 + nrm(ks[10], (DEPTH, D_MODEL), 0.02),
        "w_gate_up": nrm(ks[11], (DEPTH, D_MODEL, 2 * D_FF), D_MODEL ** -0.5),
        "w_down": nrm(ks[12], (DEPTH, D_FF, D_MODEL), D_FF ** -0.5),
    }


def reference(x_prompt, x_sample, attn_norm_g, w_qkv, q_norm_g, k_norm_g, attn_sinks, w_o_attn,
              fourier_norm_g, w_fourier_out, ffn_norm_g, w_gate_up, w_down):
    y_prompt = trunk(x_prompt, attn_norm_g, w_qkv, q_norm_g, k_norm_g, attn_sinks, w_o_attn,
                     fourier_norm_g, w_fourier_out, ffn_norm_g, w_gate_up, w_down)
    y_sample = trunk(x_sample, attn_norm_g, w_qkv, q_norm_g, k_norm_g, attn_sinks, w_o_attn,
                     fourier_norm_g, w_fourier_out, ffn_norm_g, w_gate_up, w_down)
    return (y_prompt, y_sample)
```

```python
import os
import numpy as np
import ml_dtypes
from contextlib import ExitStack
import concourse.bass as bass
import concourse.mybir as mybir
from concourse.bass_utils import run_bass_kernel_spmd

F32 = mybir.dt.float32
BF16 = mybir.dt.bfloat16
AF = mybir.ActivationFunctionType
ALU = mybir.AluOpType
NPBF = ml_dtypes.bfloat16

NCORE = 8
D = 1024
NT = 4096
DFF = 2816
EPS = 1e-6
DBG = int(os.environ.get('KDBG', '9'))
PHASES = [('attn', 0), ('ffn', 0), ('four', 0), ('ffn', 1), ('attn', 1), ('ffn', 2), ('four', 1), ('ffn', 3)]
SEGS = [(0, 2048, 16384), (2048, 1024, 8192), (3072, 1024, 8192)]


class Prog:
    def __init__(self):
        self.ops = []
        self.lastw = {}
        self.readers = {}
        self.stream_last = {}
        self.bar = None

    def barrier(self, fn):
        i = len(self.ops)
        lo = 0 if self.bar is None else self.bar
        deps = set(range(lo, i))
        if 'bar' in self.stream_last:
            deps.add(self.stream_last['bar'])
        self.ops.append(dict(eng='sp', fn=fn, deps=deps, stream='bar', inc=16))
        self.stream_last['bar'] = i
        self.bar = i
        return i

    def op(self, eng, fn, reads=(), writes=(), stream=None, inc=16):
        i = len(self.ops)
        deps = set()
        if self.bar is not None:
            deps.add(self.bar)
        for k in list(reads) + list(writes):
            if k in self.lastw:
                deps.add(self.lastw[k])
        for k in writes:
            deps.update(self.readers.get(k, ()))
        if stream is not None and stream in self.stream_last:
            deps.add(self.stream_last[stream])
        self.ops.append(dict(eng=eng, fn=fn, deps=deps, stream=stream, inc=inc))
        for k in writes:
            self.lastw[k] = i
            self.readers[k] = []
        for k in reads:
            self.readers.setdefault(k, []).append(i)
        if stream is not None:
            self.stream_last[stream] = i
        return i

    def emit(self, nc, es):
        ops = self.ops
        engs = ['pe', 'act', 'dve', 'pool', 'sp']
        esem = {e: es.enter_context(nc.semaphore("sem_" + e)) for e in engs}
        ssem = {}
        for o in ops:
            if o['stream'] is not None and o['stream'] not in ssem:
                ssem[o['stream']] = es.enter_context(nc.semaphore("st_" + o['stream']))
        needed = [False] * len(ops)
        for i, o in enumerate(ops):
            for d in o['deps']:
                if ops[d]['stream'] is None and ops[d]['eng'] == 'pe' and o['eng'] == 'pe' and o['stream'] is None:
                    continue
                needed[d] = True
        cnt = {e: 0 for e in engs}
        scnt = {s: 0 for s in ssem}
        sig = [None] * len(ops)
        for i, o in enumerate(ops):
            if o['stream'] is not None:
                scnt[o['stream']] += o['inc']
                sig[i] = (ssem[o['stream']], scnt[o['stream']])
            elif needed[i]:
                cnt[o['eng']] += 1
                sig[i] = (esem[o['eng']], cnt[o['eng']])
        per = {e: [i for i, o in enumerate(ops) if o['eng'] == e] for e in engs}
        with nc.Block() as block:
            def run(eng_name, eng):
                waited = {}
                for i in per[eng_name]:
                    o = ops[i]
                    need = {}
                    for d in o['deps']:
                        if sig[d] is None:
                            continue
                        if ops[d]['stream'] is None and ops[d]['eng'] == 'pe' and eng_name == 'pe' and o['stream'] is None:
                            continue
                        s, v = sig[d]
                        key = id(s)
                        if waited.get(key, 0) >= v:
                            continue
                        if key not in need or need[key][1] < v:
                            need[key] = (s, v)
                    for key, (s, v) in need.items():
                        eng.wait_ge(s, v)
                        waited[key] = v
                    ins = o['fn'](eng)
                    if sig[i] is not None and ins is not None:
                        if o['stream'] is not None:
                            if o['inc'] == 1:
                                ins.then_inc(sig[i][0])
                            else:
                                ins.then_inc(sig[i][0], o['inc'])
                        else:
                            ins.then_inc(sig[i][0], 1)

            @block.tensor
            def _(e):
                run('pe', e)

            @block.scalar
            def _(e):
                run('act', e)

            @block.vector
            def _(e):
                run('dve', e)

            @block.gpsimd
            def _(e):
                run('pool', e)

            @block.sync
            def _(e):
                run('sp', e)


def build_program():
    nc = bass.Bass("TRN2", target_bir_lowering=False)
    P = Prog()
    es = ExitStack()

    def din(name, shape, dt=F32):
        return nc.dram_tensor(name, list(shape), dt, kind="ExternalInput").ap()

    xT_in = din("xT_in", [D, NT])
    w_qkv = din("w_qkv", [2, D, 1536])
    w_o = din("w_o", [2, D, D])
    w_f = din("w_f", [2, D, D])
    w_gu = din("w_gu", [4, D, 2 * DFF])
    w_d = din("w_d", [4, DFF, D])
    gcols_d = din("gcols", [128, 64])
    qkg_d = din("qkg", [128, 4])
    sink_d = din("sinkrow", [1, 2 * 2048])
    sel_d = din("sel", [128, 16])
    cos_d = din("cosT", [128, NT])
    sin_d = din("sinT", [128, NT])
    cmat_d = din("cmat", [128, 4 * 128], BF16)
    masks_d = din("masks", [128, 8 * 512], BF16)
    f1_d = din("f1", [128, 2 * 256], BF16)
    g_d = din("gtab", [128, 2 * 128 * 64], BF16)
    ctab_d = din("ctab", [128, 2 * 8 * 128], BF16)
    yT_out = nc.dram_tensor("yT_out", [D, NT], F32, kind="ExternalOutput").ap()

    xT_d = nc.dram_tensor("xT_d", [D, NT], F32).ap()
    ag_send = nc.dram_tensor("ag_send", [8 * NT, 128], BF16).ap()
    ag_recv = nc.dram_tensor("ag_recv", [8 * 8 * NT, 128], BF16).ap()
    h_send = nc.dram_tensor("h_send", [4 * 128, 1024], BF16).ap()
    h_recv = nc.dram_tensor("h_recv", [8 * 4 * 128, 1024], BF16).ap()
    h_send2 = nc.dram_tensor("h_send2", [2 * 128, 1024], BF16).ap()
    h_recv2 = nc.dram_tensor("h_recv2", [8 * 2 * 128, 1024], BF16).ap()

    def sb(name, shape, dt):
        return es.enter_context(nc.sbuf_tensor(name, list(shape), dt))

    xb = sb("xb", [128, 8, 1024], F32)
    tmpA = sb("tmpA", [128, 512], F32)
    tmpB = sb("tmpB", [128, 512], F32)
    tmpC = sb("tmpC", [128, 512], F32)
    cosb = sb("cosb", [128, 512], F32)
    sinb = sb("sinb", [128, 512], F32)
    gcols = sb("gcols_s", [128, 64], F32)
    qkg = sb("qkg_s", [128, 4], F32)
    sel = sb("sel_s", [128, 16], F32)
    epsc = sb("epsc", [128, 1], F32)
    cmat = sb("cmat_s", [128, 512], BF16)
    ARENA = 82432
    arena = sb("arena", [128, ARENA], BF16)
    ident = cmat[:, 0:128]
    ones = cmat[:, 128:256]
    onesblk = cmat[:, 256:384]
    Rm = cmat[:, 384:512]

    ps = [es.enter_context(nc.psum_tensor("ps%d" % i, [128, 512], F32)) for i in range(7)]
    psT = es.enter_context(nc.psum_tensor("psT", [128, 1024], BF16))

    class Carve:
        def __init__(self):
            self.off = 0

        def take(self, n):
            v = arena[:, self.off:self.off + n]
            self.off += n
            assert self.off <= ARENA, self.off
            return v

    def xview(t):
        return t.rearrange("(k p) t -> p k t", p=128)

    P.op('sp', lambda e: e.dma_start(out=gcols[:], in_=gcols_d), writes=['gcols'], stream='c0')
    P.op('sp', lambda e: e.dma_start(out=qkg[:], in_=qkg_d), writes=['qkg'], stream='c1')
    P.op('sp', lambda e: e.dma_start(out=sel[:], in_=sel_d), writes=['sel'], stream='c2')
    P.op('sp', lambda e: e.dma_start(out=cmat[:], in_=cmat_d), writes=['cmat'], stream='c3')
    P.op('dve', lambda e: e.memset(epsc[:], float(EPS)), writes=['epsc'])
    state = {'xsrc': xT_in}

    def barrier():
        P.barrier(lambda e: e.dma_start(out=h_send2[0:1, 0:16], in_=h_send2[1:2, 0:16]))

    def load_x(tb, col0=0):
        src = xview(state['xsrc'])[:, :, tb * 512:(tb + 1) * 512]
        P.op('sp', lambda e: e.dma_start(out=xb[:, :, col0:col0 + 512], in_=src),
             reads=[('x', tb)], writes=[('xb', col0)], stream='xb%d' % col0)

    def store_x(tb, col0=0, dst=None):
        d = xview(dst if dst is not None else xT_d)[:, :, tb * 512:(tb + 1) * 512]
        P.op('sp', lambda e: e.dma_start(out=d, in_=xb[:, :, col0:col0 + 512]),
             reads=[('xb', col0)], writes=[('x', tb)], stream='xs%d' % col0)

    def rmsnorm(gidx, xn, sqb, col0=0, xnkey='xn'):
        xn3 = xn.rearrange("p (k t) -> p k t", k=8)
        sq3 = sqb.rearrange("p (k t) -> p k t", k=8)
        P.op('act', lambda e: e.activation(out=sq3, in_=xb[:, :, col0:col0 + 512], func=AF.Square),
             reads=[('xb', col0)], writes=['sqb'])
        for k in range(8):
            P.op('pe', lambda e, k=k: e.matmul(ps[0][:], ones, sq3[:, k, :], start=(k == 0), stop=(k == 7)),
                 reads=['sqb', 'cmat'], writes=[('ps', 0)])
        P.op('act', lambda e: e.activation(out=tmpA[:], in_=ps[0][:], func=AF.Sqrt, bias=epsc[:, 0:1], scale=1.0 / D),
             reads=[('ps', 0), 'epsc'], writes=['tmpA'])
        P.op('dve', lambda e: e.reciprocal(out=tmpA[:], in_=tmpA[:]), reads=['tmpA'], writes=['tmpA'])
        for k in range(8):
            P.op('dve', lambda e, k=k: e.scalar_tensor_tensor(
                out=xn3[:, k, :], in0=xb[:, k, col0:col0 + 512], scalar=gcols[:, gidx * 8 + k:gidx * 8 + k + 1],
                in1=tmpA[:], op0=ALU.mult, op1=ALU.mult),
                reads=[('xb', col0), 'tmpA', 'gcols'], writes=[xnkey])

    def ffn(l, last):
        barrier()
        cv = Carve()
        xn = cv.take(8 * 1024)
        sqb = cv.take(8 * 512)
        act = cv.take(22 * 1024).rearrange("p (f t) -> p f t", f=22)
        wd = cv.take(22 * 1024).rearrange("p (f n) -> p f n", f=22)
        wgu = [cv.take(8 * 512).rearrange("p (k n) -> p k n", k=8) for _ in range(2)]
        xn3 = xn.rearrange("p (s k t) -> p s k t", s=2, k=8)
        wdsrc = w_d[l].rearrange("(f p) n -> p f n", p=128)
        for q in range(2):
            P.op('pool', lambda e, q=q: e.dma_start(out=wd[:, q * 11:(q + 1) * 11, :], in_=wdsrc[:, q * 11:(q + 1) * 11, :]),
                 writes=['wd'], stream='wd')
        gusrc = w_gu[l].rearrange("(k p) n -> p k n", p=128)
        for sbk in range(4):
            for s in range(2):
                tb = sbk * 2 + s
                load_x(tb, s * 512)
                rmsnorm(4 + l, xn[:, s * 4096:(s + 1) * 4096], sqb, s * 512, xnkey=('xn', s))
            for fp in range(11):
                slot = fp % 2
                w = wgu[slot]
                P.op('pool', lambda e, w=w, fp=fp: e.dma_start(out=w[:, :, 0:256], in_=gusrc[:, :, fp * 256:(fp + 1) * 256]),
                     writes=[('wgu', slot)], stream='wgu%d' % slot)
                P.op('pool', lambda e, w=w, fp=fp: e.dma_start(out=w[:, :, 256:512],
                                                               in_=gusrc[:, :, DFF + fp * 256:DFF + (fp + 1) * 256]),
                     writes=[('wgu', slot)], stream='wgu%d' % slot)
                for fc in range(2):
                    f = fp * 2 + fc
                    for s in range(2):
                        pg, pu = (1, 2) if (f * 2 + s) % 2 == 0 else (3, 4)
                        for k in range(8):
                            P.op('pe', lambda e, k=k, w=w, fc=fc, s=s, pg=pg: e.matmul(
                                ps[pg][:], w[:, k, fc * 128:(fc + 1) * 128], xn3[:, s, k, :], start=(k == 0), stop=(k == 7)),
                                reads=[('wgu', slot), ('xn', s)], writes=[('ps', pg)])
                        for k in range(8):
                            P.op('pe', lambda e, k=k, w=w, fc=fc, s=s, pu=pu: e.matmul(
                                ps[pu][:], w[:, k, 256 + fc * 128:256 + (fc + 1) * 128], xn3[:, s, k, :], start=(k == 0), stop=(k == 7)),
                                reads=[('wgu', slot), ('xn', s)], writes=[('ps', pu)])
                        tmp = tmpB if pg == 1 else tmpC
                        tk = 'tmpB' if pg == 1 else 'tmpC'
                        P.op('act', lambda e, pg=pg, tmp=tmp: e.activation(out=tmp[:], in_=ps[pg][:], func=AF.Silu),
                             reads=[('ps', pg)], writes=[tk])
                        P.op('dve', lambda e, pu=pu, tmp=tmp, f=f, s=s: e.tensor_tensor(
                            out=act[:, f, s * 512:(s + 1) * 512], in0=ps[pu][:], in1=tmp[:], op=ALU.mult),
                            reads=[('ps', pu), tk], writes=[('act', s)])
            for s in range(2):
                tb = sbk * 2 + s
                for dc in range(8):
                    pd = 5 + (dc % 2)
                    for f in range(22):
                        P.op('pe', lambda e, f=f, dc=dc, s=s, pd=pd: e.matmul(
                            ps[pd][:], wd[:, f, dc * 128:(dc + 1) * 128], act[:, f, s * 512:(s + 1) * 512],
                            start=(f == 0), stop=(f == 21)),
                            reads=['wd', ('act', s)], writes=[('ps', pd)])
                    P.op('dve', lambda e, dc=dc, s=s, pd=pd: e.tensor_tensor(
                        out=xb[:, dc, s * 512:(s + 1) * 512], in0=ps[pd][:], in1=xb[:, dc, s * 512:(s + 1) * 512], op=ALU.add),
                        reads=[('ps', pd), ('xb', s * 512)], writes=[('xb', s * 512)])
                store_x(tb, s * 512, dst=(yT_out if last else None))
        state['xsrc'] = xT_d

    def attention(la):
        barrier()
        cv = Carve()
        xn = cv.take(8 * 512)
        sqb = cv.take(8 * 512)
        Wq = cv.take(8 * 1024).rearrange("p (k n) -> p k n", k=8)
        Wkd = cv.take(8 * 512).rearrange("p (k n) -> p k n", k=8)
        Wv = cv.take(8 * 256).rearrange("p (k n) -> p k n", k=8)
        Qh = cv.take(8 * 2048).rearrange("p (c t) -> p c t", c=8)
        Klo = cv.take(4 * 2048).rearrange("p (h t) -> p h t", h=4)
        Khi = cv.take(4 * 2048).rearrange("p (h t) -> p h t", h=4)
        Vh = cv.take(16 * 512).rearrange("p (b n) -> p b n", b=16)
        halo = cv.take(4 * 1536).rearrange("p (i n) -> p i n", i=4)
        cand = cv.take(8 * 512).rearrange("p (j n) -> p j n", j=8)
        PT = cv.take(3 * 512).rearrange("p (j n) -> p j n", j=3)
        OT = xn.rearrange("p (c t) -> p c t", c=8)
        masks = cv.take(8 * 512).rearrange("p (m n) -> p m n", m=8)
        esink = cv.take(2048)
        qbf = cv.take(512)
        sqc = cv.take(512)
        xn3 = xn.rearrange("p (k t) -> p k t", k=8)
        P.op('sp', lambda e: e.dma_start(out=masks, in_=masks_d.rearrange("p (m n) -> p m n", m=8)), writes=['masks'], stream='c0')
        P.op('sp', lambda e: e.dma_start(out=tmpB[0:1, :], in_=sink_d[0:1, la * 2048:la * 2048 + 512]), writes=['tmpB'], stream='c1')
        for h in range(4):
            if h > 0:
                P.op('sp', lambda e, h=h: e.dma_start(out=tmpB[0:1, :], in_=sink_d[0:1, la * 2048 + h * 512:la * 2048 + (h + 1) * 512]),
                     writes=['tmpB'], stream='c1')
            P.op('act', lambda e, h=h: e.activation(out=esink[0:1, h * 512:(h + 1) * 512], in_=tmpB[0:1, :], func=AF.Exp),
                 reads=['tmpB'], writes=['esink'])
        qsrc = w_qkv[la].rearrange("(k p) n -> p k n", p=128)
        KK = [('K', i) for i in range(4)]
        HK = [('halo', i) for i in range(4)]
        P.op('pool', lambda e: e.memset(Klo[64:128, :, :], 0.0), writes=KK)
        P.op('pool', lambda e: e.memset(Khi[0:64, :, :], 0.0), writes=KK)
        P.op('pool', lambda e: e.memset(halo[64:128, :, 0:512], 0.0), writes=HK)
        P.op('pool', lambda e: e.memset(halo[0:64, :, 512:1024], 0.0), writes=HK)

        def load_w1():
            P.op('pool', lambda e: e.dma_start(out=Wq, in_=qsrc[:, :, 0:1024]), writes=['Wq'], stream='wq')
            for u in range(2):
                for kc in range(8):
                    P.op('pool', lambda e, u=u, kc=kc: e.dma_start(
                        out=Wkd[:, kc, :].rearrange("p (h u d) -> p h u d", h=4, u=2)[:, :, u, :],
                        in_=qsrc[:, kc, 1024:1280].rearrange("p (h d) -> p h d", h=4)), writes=['Wkd'], stream='wk')
            P.op('pool', lambda e: e.dma_start(out=Wv, in_=qsrc[:, :, 1280:1536]), writes=['Wv'], stream='wv')

        def qk_post(pp, gcol, dst, dkey):
            P.op('act', lambda e: e.activation(out=sqc, in_=ps[pp][:], func=AF.Square), reads=[('ps', pp)], writes=['sqc'])
            P.op('pe', lambda e: e.matmul(ps[2][:], onesblk, sqc, start=True, stop=True), reads=['sqc', 'cmat'], writes=[('ps', 2)])
            P.op('act', lambda e: e.activation(out=tmpA[:], in_=ps[2][:], func=AF.Sqrt, bias=epsc[:, 0:1], scale=1.0 / 64),
                 reads=[('ps', 2), 'epsc'], writes=['tmpA'])
            P.op('dve', lambda e: e.reciprocal(out=tmpA[:], in_=tmpA[:]), reads=['tmpA'], writes=['tmpA'])
            P.op('dve', lambda e: e.scalar_tensor_tensor(out=qbf, in0=ps[pp][:], scalar=qkg[:, gcol:gcol + 1], in1=tmpA[:],
                                                         op0=ALU.mult, op1=ALU.mult),
                 reads=[('ps', pp), 'tmpA', 'qkg'], writes=['qbf'])
            P.op('pe', lambda e: e.matmul(ps[3][:], Rm, qbf, start=True, stop=True), reads=['qbf', 'cmat'], writes=[('ps', 3)])
            P.op('dve', lambda e: e.tensor_tensor(out=tmpB[:], in0=qbf, in1=cosb[:], op=ALU.mult),
                 reads=['qbf', 'cos'], writes=['tmpB'])
            P.op('dve', lambda e: e.tensor_tensor(out=tmpC[:], in0=ps[3][:], in1=sinb[:], op=ALU.mult),
                 reads=[('ps', 3), 'sin'], writes=['tmpC'])
            if isinstance(dst, tuple):
                dlo, dhi = dst
                P.op('pool', lambda e: e.tensor_tensor(out=dlo, in0=tmpB[0:64, :], in1=tmpC[0:64, :], op=ALU.add),
                     reads=['tmpB', 'tmpC'], writes=[dkey])
                P.op('pool', lambda e: e.tensor_tensor(out=dhi, in0=tmpB[64:128, :], in1=tmpC[64:128, :], op=ALU.add),
                     reads=['tmpB', 'tmpC'], writes=[dkey])
            else:
                P.op('pool', lambda e: e.tensor_tensor(out=dst, in0=tmpB[:], in1=tmpC[:], op=ALU.add),
                     reads=['tmpB', 'tmpC'], writes=[dkey])

        for hf in range(2):
            segs = [(0, 16, 0)] if hf == 0 else [(0, 8, 1), (8, 8, 2)]
            load_w1()
            for tbl in range(4):
                tb = hf * 4 + tbl
                load_x(tb, 0)
                rmsnorm(la, xn, sqb, 0)
                P.op('sp', lambda e, tb=tb: e.dma_start(out=cosb[:], in_=cos_d[:, tb * 512:(tb + 1) * 512]), writes=['cos'], stream='cos')
                P.op('sp', lambda e, tb=tb: e.dma_start(out=sinb[:], in_=sin_d[:, tb * 512:(tb + 1) * 512]), writes=['sin'], stream='sin')
                for pc in range(12):
                    pp = 1 if pc % 2 == 0 else 4
                    for k in range(8):
                        if pc < 8:
                            P.op('pe', lambda e, k=k, pc=pc, pp=pp: e.matmul(ps[pp][:], Wq[:, k, pc * 128:(pc + 1) * 128], xn3[:, k, :],
                                                                            start=(k == 0), stop=(k == 7)),
                                 reads=['Wq', 'xn'], writes=[('ps', pp)])
                        else:
                            h = pc - 8
                            P.op('pe', lambda e, k=k, h=h, pp=pp: e.matmul(ps[pp][:], Wkd[:, k, h * 128:(h + 1) * 128], xn3[:, k, :],
                                                                          start=(k == 0), stop=(k == 7)),
                                 reads=['Wkd', 'xn'], writes=[('ps', pp)])
                    if pc < 8:
                        qk_post(pp, 2 * la, Qh[:, pc, tbl * 512:(tbl + 1) * 512], ('Q', tbl))
                    else:
                        qk_post(pp, 2 * la + 1, (Klo[0:64, pc - 8, tbl * 512:(tbl + 1) * 512],
                                                 Khi[64:128, pc - 8, tbl * 512:(tbl + 1) * 512]), ('K', tbl))
                for sub in range(4):
                    pv = 5 + sub % 2
                    blk = tbl * 4 + sub
                    for k in range(8):
                        P.op('pe', lambda e, k=k, sub=sub, pv=pv: e.matmul(ps[pv][:, 0:256], xn3[:, k, sub * 128:(sub + 1) * 128], Wv[:, k, :],
                                                                          start=(k == 0), stop=(k == 7)),
                             reads=['Wv', 'xn'], writes=[('ps', pv)])
                    for u in range(2):
                        P.op('act', lambda e, u=u, blk=blk, pv=pv: e.activation(
                            out=Vh[:, blk, :].rearrange("p (h u d) -> p h u d", h=4, u=2)[:, :, u, :],
                            in_=ps[pv][:, 0:256].rearrange("p (h d) -> p h d", h=4), func=AF.Copy),
                            reads=[('ps', pv)], writes=[('V', tbl)])
            if DBG < 2:
                continue
            nh = 2 * len(segs)
            hs = h_send2 if hf == 0 else h_send
            hr = h_recv2 if hf == 0 else h_recv
            for si, (b0, nb, gs) in enumerate(segs):
                for side, blk in ((0, b0), (1, b0 + nb - 1)):
                    idx = si * 2 + side
                    P.op('sp', lambda e, idx=idx, blk=blk, hs=hs: e.dma_start(
                        out=hs[idx * 128:idx * 128 + 64, 0:512].rearrange("p (h t) -> p h t", h=4),
                        in_=Klo[0:64, :, blk * 128:(blk + 1) * 128]),
                        reads=[('K', blk // 4)], writes=['hsend'], stream='hs0')
                    P.op('sp', lambda e, idx=idx, blk=blk, hs=hs: e.dma_start(
                        out=hs[idx * 128 + 64:(idx + 1) * 128, 0:512].rearrange("p (h t) -> p h t", h=4),
                        in_=Khi[64:128, :, blk * 128:(blk + 1) * 128]),
                        reads=[('K', blk // 4)], writes=['hsend'], stream='hs0')
                    P.op('sp', lambda e, idx=idx, blk=blk, hs=hs: e.dma_start(
                        out=hs[idx * 128:(idx + 1) * 128, 512:1024], in_=Vh[:, blk, :]),
                        reads=[('V', blk // 4)], writes=['hsend'], stream='hs1')
            P.op('pool', lambda e, hs=hs, hr=hr: e.collective_compute(
                "AllGather", ALU.bypass, replica_groups=[list(range(NCORE))], ins=[hs.opt()], outs=[hr.opt()]),
                reads=['hsend'], writes=['hrecv'], stream='cc', inc=1)
            hr4 = hr.rearrange("(j i p) n -> p j i n", j=8, i=nh)
            for si, (b0, nb, gs) in enumerate(segs):
                for side in range(2):
                    hidx = si * 2 + side
                    srcidx = si * 2 + (1 - side)
                    for part in range(2):
                        P.op('sp', lambda e, srcidx=srcidx, hr4=hr4, part=part: e.dma_start(
                            out=cand, in_=hr4[:, :, srcidx, part * 512:(part + 1) * 512]),
                            reads=['hrecv'], writes=['cand'], stream='cand')
                        acc = qbf if part == 0 else halo[:, hidx, 1024:1536]
                        akey = 'qbf' if part == 0 else ('halo', hidx)
                        for j in range(8):
                            sc = sel[:, side * 8 + j:side * 8 + j + 1]
                            if j == 0:
                                P.op('dve', lambda e, sc=sc, acc=acc: e.tensor_scalar(out=acc, in0=cand[:, 0, :], scalar1=sc,
                                                                                       scalar2=None, op0=ALU.mult),
                                     reads=['cand', 'sel'], writes=[akey])
                            else:
                                P.op('dve', lambda e, sc=sc, acc=acc, j=j: e.scalar_tensor_tensor(
                                    out=acc, in0=cand[:, j, :], scalar=sc, in1=acc, op0=ALU.mult, op1=ALU.add),
                                    reads=['cand', 'sel', akey], writes=[akey])
                        if part == 0:
                            P.op('dve', lambda e, hidx=hidx: e.tensor_copy(out=halo[0:64, hidx, 0:512], in_=qbf[0:64, :]),
                                 reads=['qbf'], writes=[('halo', hidx)])
                            P.op('dve', lambda e, hidx=hidx: e.tensor_copy(out=halo[64:128, hidx, 512:1024], in_=qbf[64:128, :]),
                                 reads=['qbf'], writes=[('halo', hidx)])
            if DBG < 3:
                continue
            P.op('pool', lambda e: e.dma_start(out=Wq, in_=w_o[la].rearrange("(k p) n -> p k n", p=128)), writes=['Wq'], stream='wq')
            for tbl in range(4):
                tb = hf * 4 + tbl
                load_x(tb, 0)
                for qb in range(4):
                    n = tbl * 4 + qb
                    si = [i for i, (b0, nb, gs) in enumerate(segs) if b0 <= n < b0 + nb][0]
                    b0, nb, gs = segs[si]
                    for h in range(4):
                        for jj, j in enumerate((n - 1, n, n + 1)):
                            if b0 <= j < b0 + nb:
                                Ks = (Klo[:, h, j * 128:(j + 1) * 128], Khi[:, h, j * 128:(j + 1) * 128])
                                Vs = Vh[:, j, h * 128:(h + 1) * 128]
                                rk = [('K', j // 4), ('V', j // 4)]
                                m = None if j == n else (0 if j < n else 1)
                            else:
                                side = 0 if j < n else 1
                                hidx = si * 2 + side
                                Ks = (halo[:, hidx, h * 128:(h + 1) * 128], halo[:, hidx, 512 + h * 128:512 + (h + 1) * 128])
                                Vs = halo[:, hidx, 1024 + h * 128:1024 + (h + 1) * 128]
                                rk = [('halo', hidx)]
                                m = 2 + gs * 2 + side
                            pS = 1 + jj
                            for half in range(2):
                                lo = half * 64
                                P.op('pe', lambda e, Ks=Ks, lo=lo, h=h, n=n, half=half, pS=pS: e.matmul(
                                    ps[pS][:, half * 256:(half + 1) * 256], Ks[half],
                                    Qh[:, 2 * h:2 * h + 2, n * 128:(n + 1) * 128], start=True, stop=True),
                                    reads=rk + [('Q', n // 4)], writes=[('ps', pS)])
                            P.op('act', lambda e, jj=jj, pS=pS: e.activation(out=PT[:, jj, :], in_=ps[pS][:], func=AF.Exp, scale=0.125),
                                 reads=[('ps', pS)], writes=[('PT', jj)])
                            if m is not None:
                                P.op('dve', lambda e, jj=jj, m=m: e.tensor_tensor(out=PT[:, jj, :], in0=PT[:, jj, :], in1=masks[:, m, :],
                                                                                  op=ALU.mult),
                                     reads=[('PT', jj), 'masks'], writes=[('PT', jj)])
                            if DBG < 4:
                                continue
                            P.op('pe', lambda e, Vs=Vs, jj=jj: e.matmul(ps[4][:], Vs, PT[:, jj, :], start=(jj == 0), stop=(jj == 2)),
                                 reads=rk + [('PT', jj)], writes=[('ps', 4)])
                            P.op('pe', lambda e, jj=jj: e.matmul(ps[5][:], ones, PT[:, jj, :], start=(jj == 0), stop=False),
                                 reads=['cmat', ('PT', jj)], writes=[('ps', 5)])
                        if DBG < 4:
                            continue
                        if DBG >= 5:
                          P.op('pe', lambda e, h=h: e.matmul(ps[5][:], ones[0:1, :], esink[0:1, h * 512:(h + 1) * 512], start=False, stop=True),
                             reads=['cmat', 'esink'], writes=[('ps', 5)])
                        if DBG < 6:
                            continue
                        P.op('dve', lambda e: e.reciprocal(out=tmpA[:], in_=ps[5][:]), reads=[('ps', 5)], writes=['tmpA'])
                        for half in range(2):
                            lo = half * 64
                            P.op('dve', lambda e, lo=lo, half=half, h=h, qb=qb: e.tensor_tensor(
                                out=OT[lo:lo + 64, 2 * h:2 * h + 2, qb * 128:(qb + 1) * 128],
                                in0=ps[4][lo:lo + 64, half * 256:(half + 1) * 256].rearrange("p (c t) -> p c t", c=2),
                                in1=tmpA[lo:lo + 64, half * 256:(half + 1) * 256].rearrange("p (c t) -> p c t", c=2), op=ALU.mult),
                                reads=[('ps', 4), 'tmpA'], writes=['xn'])
                for dc in range(8):
                    pd = 6 if dc % 2 == 0 else 0
                    for pc in range(8):
                        P.op('pe', lambda e, pc=pc, dc=dc, pd=pd: e.matmul(ps[pd][:], Wq[:, pc, dc * 128:(dc + 1) * 128], OT[:, pc, :],
                                                                          start=(pc == 0), stop=(pc == 7)),
                             reads=['Wq', 'xn'], writes=[('ps', pd)])
                    P.op('dve', lambda e, dc=dc, pd=pd: e.tensor_tensor(out=xb[:, dc, 0:512], in0=ps[pd][:], in1=xb[:, dc, 0:512], op=ALU.add),
                         reads=[('ps', pd), ('xb', 0)], writes=[('xb', 0)])
                store_x(tb, 0)
        state['xsrc'] = xT_d

    def fourier(lf):
        barrier()
        cv = Carve()
        Xag = cv.take(128 * 128).rearrange("p (b c) -> p b c", b=128)
        U = cv.take(128 * 256)
        U3 = U.rearrange("p (c n) -> p c n", c=128)
        Wf = U[:, 0:8192].rearrange("p (k n) -> p k n", k=8)
        Vt = cv.take(2 * 2 * 2048).rearrange("p (a r t) -> p a r t", a=2, r=2)
        fT = cv.take(8 * 2048).rearrange("p (c t) -> p c t", c=8)
        G = cv.take(64 * 64).rearrange("p (k n) -> p k n", k=64)
        F1 = cv.take(512).rearrange("p (s n) -> p s n", s=2)
        Ct = cv.take(2048).rearrange("p (s i n) -> p s i n", s=2, i=8)
        xn = U[:, 8192:12288]
        sqb = U[:, 12288:16384]
        xtok = U[:, 16384:17408]
        xn3 = xn.rearrange("p (k t) -> p k t", k=8)
        P.op('sp', lambda e: e.dma_start(out=F1, in_=f1_d.rearrange("p (s n) -> p s n", s=2)), writes=['F1'], stream='c0')
        P.op('sp', lambda e: e.dma_start(out=Ct, in_=ctab_d.rearrange("p (s i n) -> p s i n", s=2, i=8)), writes=['Ct'], stream='c1')
        send3 = ag_send.rearrange("(c t) k -> t c k", c=8)
        for tb in range(8):
            load_x(tb, 0)
            rmsnorm(2 + lf, xn, sqb, 0)
            for sub in range(4):
                for k in range(8):
                    P.op('pe', lambda e, k=k, sub=sub: e.transpose(psT[:, k * 128:(k + 1) * 128], xn3[:, k, sub * 128:(sub + 1) * 128], ident),
                         reads=['xn', 'cmat'], writes=['psT'])
                P.op('act', lambda e: e.activation(out=xtok, in_=psT[:], func=AF.Copy), reads=['psT'], writes=['xtok'])
                t0 = tb * 512 + sub * 128
                P.op('sp', lambda e, t0=t0: e.dma_start(out=send3[t0:t0 + 128, :, :], in_=xtok.rearrange("p (c k) -> p c k", c=8)),
                     reads=['xtok'], writes=['agsend'], stream='ags')
        P.op('pool', lambda e: e.collective_compute("AllGather", ALU.bypass, replica_groups=[list(range(NCORE))],
                                                    ins=[ag_send.opt()], outs=[ag_recv.opt()]),
             reads=['agsend'], writes=['agrecv'], stream='cc', inc=1)
        recv4 = ag_recv.rearrange("(j c t) k -> j c t k", j=8, c=8)
        wfsrc = w_f[lf].rearrange("(k p) n -> p k n", p=128)
        for st in range(2):
            for g in range(4):
                for cl in range(2):
                    cc = 2 * g + cl
                    nd = 0
                    for j in range(8):
                        if st == 0:
                            P.op('sp', lambda e, j=j, cc=cc: e.dma_start(
                                out=Xag[16 * j:16 * j + 16, :, :],
                                in_=recv4[j, cc, 0:2048, :].rearrange("(a b) k -> a b k", a=16)),
                                reads=['agrecv'], writes=['Xag'], stream='xag%d' % (nd % 4))
                            nd += 1
                        else:
                            for s in range(2):
                                P.op('sp', lambda e, j=j, cc=cc, s=s: e.dma_start(
                                    out=Xag[64 * s + 8 * j:64 * s + 8 * j + 8, :, :],
                                    in_=recv4[j, cc, 2048 + 1024 * s:3072 + 1024 * s, :].rearrange("(a b) k -> a b k", a=8)),
                                    reads=['agrecv'], writes=['Xag'], stream='xag%d' % (nd % 4))
                                nd += 1
                    for c2 in range(64):
                        pu = 1 + c2 % 2
                        for u in range(2):
                            c = c2 * 2 + u
                            P.op('pe', lambda e, c=c, u=u, pu=pu, st=st: e.matmul(ps[pu][:, u * 256:(u + 1) * 256], Xag[:, :, c], F1[:, st, :],
                                                                                  start=True, stop=True),
                                 reads=['Xag', 'F1'], writes=[('ps', pu)])
                        eng = 'act' if c2 % 2 == 0 else 'dve'
                        if eng == 'act':
                            P.op('act', lambda e, c2=c2, pu=pu: e.activation(out=U[:, c2 * 512:(c2 + 1) * 512], in_=ps[pu][:], func=AF.Copy),
                                 reads=[('ps', pu)], writes=['U'])
                        else:
                            P.op('dve', lambda e, c2=c2, pu=pu: e.tensor_copy(out=U[:, c2 * 512:(c2 + 1) * 512], in_=ps[pu][:]),
                                 reads=[('ps', pu)], writes=['U'])
                    for k16 in range(8):
                        pv = 3 + k16 % 2
                        if k16 % 4 == 0:
                            gh = k16 // 4
                            P.op('sp', lambda e, st=st, gh=gh: e.dma_start(
                                out=G, in_=g_d[:, st * 8192 + gh * 4096:st * 8192 + (gh + 1) * 4096].rearrange("p (k n) -> p k n", k=64)),
                                writes=['G'], stream='gld')
                        for kq in range(16):
                            kk = k16 * 16 + kq
                            P.op('pe', lambda e, kk=kk, kq=kq, pv=pv: e.matmul(ps[pv][:, kq * 32:(kq + 1) * 32], U3[:, :, kk], G[:, kk % 64, 0:32],
                                                                              start=True, stop=False),
                                 reads=['U', 'G'], writes=[('ps', pv)])
                            P.op('pe', lambda e, kk=kk, kq=kq, pv=pv: e.matmul(ps[pv][:, kq * 32:(kq + 1) * 32], U3[:, :, 128 + kk], G[:, kk % 64, 32:64],
                                                                              start=False, stop=True),
                                 reads=['U', 'G'], writes=[('ps', pv)])
                        for r in range(2):
                            src = ps[pv][:].rearrange("p (k r l) -> p r l k", k=16, r=2, l=16)[:, r, :, :]
                            if st == 0:
                                dst = Vt[:, cl, r, :].rearrange("p (l k) -> p l k", l=16)[:, :, k16 * 16:(k16 + 1) * 16]
                            else:
                                sq_, k1_0 = divmod(k16 * 16, 64)
                                dst = Vt[:, cl, r, sq_ * 1024:(sq_ + 1) * 1024].rearrange("p (l k) -> p l k", l=16)[:, :, k1_0:k1_0 + 16]
                            if r == 0:
                                P.op('act', lambda e, src=src, dst=dst: e.activation(out=dst, in_=src, func=AF.Copy),
                                     reads=[('ps', pv)], writes=[('Vt', cl)])
                            else:
                                P.op('dve', lambda e, src=src, dst=dst: e.tensor_copy(out=dst, in_=src),
                                     reads=[('ps', pv)], writes=[('Vt', cl)])
                for c2l in range(2):
                    for tq in range(4):
                        first = True
                        for cl in range(2):
                            for r in range(2):
                                i = (cl * 2 + r) * 2 + c2l
                                last_ = (cl == 1 and r == 1)
                                P.op('pe', lambda e, i=i, cl=cl, r=r, tq=tq, st=st, first=first, last_=last_: e.matmul(
                                    ps[5][:], Ct[:, st, i, :], Vt[:, cl, r, tq * 512:(tq + 1) * 512], start=first, stop=last_),
                                    reads=['Ct', ('Vt', cl)], writes=[('ps', 5)])
                                first = False
                        P.op('act', lambda e, g=g, c2l=c2l, tq=tq: e.activation(out=fT[:, 2 * g + c2l, tq * 512:(tq + 1) * 512], in_=ps[5][:], func=AF.Copy),
                             reads=[('ps', 5)], writes=['fT'])
            P.op('pool', lambda e: e.dma_start(out=Wf, in_=wfsrc), writes=['U'], stream='wq')
            for tq in range(4):
                tb = st * 4 + tq
                load_x(tb, 0)
                for dc in range(8):
                    pd = 6 if dc % 2 == 0 else 0
                    for c in range(8):
                        P.op('pe', lambda e, c=c, dc=dc, pd=pd, tq=tq: e.matmul(ps[pd][:], Wf[:, c, dc * 128:(dc + 1) * 128], fT[:, c, tq * 512:(tq + 1) * 512],
                                                                              start=(c == 0), stop=(c == 7)),
                             reads=['U', 'fT'], writes=[('ps', pd)])
                    P.op('dve', lambda e, dc=dc, pd=pd: e.tensor_tensor(out=xb[:, dc, 0:512], in0=ps[pd][:], in1=xb[:, dc, 0:512], op=ALU.add),
                         reads=[('ps', pd), ('xb', 0)], writes=[('xb', 0)])
                store_x(tb, 0)
        state['xsrc'] = xT_d

    for pi, (kind, idx) in enumerate(PHASES):
        if kind == 'attn':
            attention(idx)
        elif kind == 'four':
            fourier(idx)
        else:
            ffn(idx, last=False)
    barrier()
    for tb in range(8):
        load_x(tb, 0)
        store_x(tb, 0, dst=yT_out)
    barrier()
    P.op('sp', lambda e: e.dma_start(out=h_send2[2:3, 0:16], in_=h_send2[3:4, 0:16]), stream='fin')
    barrier()
    with es:
        P.emit(nc, es)
    return nc


def _tables():
    c = {}
    ident = np.eye(128, dtype=np.float32)
    ones = np.ones((128, 128), np.float32)
    onesblk = np.zeros((128, 128), np.float32)
    onesblk[:64, :64] = 1
    onesblk[64:, 64:] = 1
    Rm = np.zeros((128, 128), np.float32)
    for hh in range(2):
        for d in range(32):
            Rm[hh * 64 + d + 32, hh * 64 + d] = -1.0
            Rm[hh * 64 + d, hh * 64 + d + 32] = 1.0
    c['cmat'] = np.concatenate([ident, ones, onesblk, Rm], axis=1).astype(NPBF)
    cc = np.arange(256)
    ang = 2 * np.pi * np.outer(cc, cc) / 256.0
    Cc, Sc = np.cos(ang), np.sin(ang)
    ct = np.zeros((128, 2, 8, 128), np.float64)
    for st, S in enumerate((16384, 8192)):
        sc = 1.0 / np.sqrt(S * 256.0)
        for cl in range(2):
            for r in range(2):
                for c2 in range(2):
                    M = Cc if r == 0 else Sc
                    ct[:, st, (cl * 2 + r) * 2 + c2, :] = sc * M[cl * 128:(cl + 1) * 128, c2 * 128:(c2 + 1) * 128]
    c['ctab'] = ct.reshape(128, -1).astype(NPBF)
    f1 = np.zeros((128, 2, 2, 128), np.float64)
    a = np.arange(128)
    ang = 2 * np.pi * np.outer(a, a) / 128.0
    f1[:, 0, 0, :] = np.cos(ang)
    f1[:, 0, 1, :] = -np.sin(ang)
    a64 = np.arange(64)
    ang = 2 * np.pi * np.outer(a64, a64) / 64.0
    for s in range(2):
        f1[s * 64:(s + 1) * 64, 1, 0, s * 64:(s + 1) * 64] = np.cos(ang)
        f1[s * 64:(s + 1) * 64, 1, 1, s * 64:(s + 1) * 64] = -np.sin(ang)
    c['f1'] = f1.reshape(128, -1).astype(NPBF)
    return c


def _core_tables(c):
    t = {}
    b = np.arange(128)[:, None, None]
    l = np.arange(16)[None, None, :]
    g = np.zeros((128, 2, 128, 64), np.float64)
    k1 = np.arange(128)[None, :, None]
    ang = 2 * np.pi * ((b * (k1 + 128 * (16 * c + l))) % 16384) / 16384.0
    Gr, Gi = np.cos(ang), -np.sin(ang)
    g[:, 0, :, 0:16], g[:, 0, :, 16:32], g[:, 0, :, 32:48], g[:, 0, :, 48:64] = Gr, Gi, -Gi, Gr
    k1 = (np.arange(128) % 64)[None, :, None]
    ang = 2 * np.pi * ((b * (k1 + 64 * (16 * c + l))) % 8192) / 8192.0
    Gr, Gi = np.cos(ang), -np.sin(ang)
    g[:, 1, :, 0:16], g[:, 1, :, 16:32], g[:, 1, :, 32:48], g[:, 1, :, 48:64] = Gr, Gi, -Gi, Gr
    t['gtab'] = g.reshape(128, -1).astype(NPBF)
    pos = np.concatenate([2048 * c + np.arange(2048), 1024 * c + np.arange(1024), 1024 * c + np.arange(1024)]).astype(np.float32)
    half = 32
    inv = (np.float32(10000.0) ** (-np.arange(half, dtype=np.float32) / half)).astype(np.float32)
    angf = pos[None, :] * inv[:, None]
    d = np.arange(128) % 32
    t['cosT'] = np.cos(angf)[d].astype(np.float32)
    t['sinT'] = np.sin(angf)[d].astype(np.float32)
    kk = np.arange(128)[:, None]
    qq = np.arange(128)[None, :]
    tri_ge = (kk >= qq).astype(np.float32)
    tri_le = (kk <= qq).astype(np.float32)
    m = np.zeros((128, 8, 4, 128), np.float32)
    m[:, 0] = tri_ge[:, None, :]
    m[:, 1] = tri_le[:, None, :]
    for gs in range(3):
        if c > 0:
            m[:, 2 + gs * 2 + 0] = tri_ge[:, None, :]
        if c < 7:
            m[:, 2 + gs * 2 + 1] = tri_le[:, None, :]
    t['masks'] = m.reshape(128, -1).astype(NPBF)
    s = np.zeros((128, 16), np.float32)
    if c > 0:
        s[:, c - 1] = 1.0
    if c < 7:
        s[:, 8 + c + 1] = 1.0
    t['sel'] = s
    return t


_NC = None


def kernel(x_prompt, x_sample, attn_norm_g, w_qkv, q_norm_g, k_norm_g, attn_sinks, w_o_attn,
           fourier_norm_g, w_fourier_out, ffn_norm_g, w_gate_up, w_down):
    global _NC
    if _NC is None:
        _NC = build_program()
    nc = _NC
    f32 = np.float32
    common = _tables()
    gl = [attn_norm_g[0], attn_norm_g[1], fourier_norm_g[0], fourier_norm_g[1],
          ffn_norm_g[0], ffn_norm_g[1], ffn_norm_g[2], ffn_norm_g[3]]
    gcols = np.concatenate([np.asarray(g, f32).reshape(8, 128).T for g in gl], axis=1)
    qkg = np.stack([np.tile(np.asarray(q_norm_g[0], f32), 2), np.tile(np.asarray(k_norm_g[0], f32), 2),
                    np.tile(np.asarray(q_norm_g[1], f32), 2), np.tile(np.asarray(k_norm_g[1], f32), 2)], axis=1)
    sr = np.zeros((2, 4, 4, 128), f32)
    for la in range(2):
        for h in range(4):
            for ci, gq in enumerate((0, 2, 1, 3)):
                sr[la, h, ci, :] = attn_sinks[la][4 * h + gq]
    sinkrow = sr.reshape(1, -1)
    shared = dict(w_qkv=np.ascontiguousarray(w_qkv, f32), w_o=np.ascontiguousarray(w_o_attn, f32),
                  w_f=np.ascontiguousarray(w_fourier_out, f32), w_gu=np.ascontiguousarray(w_gate_up, f32),
                  w_d=np.ascontiguousarray(w_down, f32), gcols=np.ascontiguousarray(gcols), qkg=np.ascontiguousarray(qkg),
                  sinkrow=np.ascontiguousarray(sinkrow), cmat=common['cmat'], f1=common['f1'], ctab=common['ctab'])
    in_maps = []
    for c in range(NCORE):
        xt = np.concatenate([x_prompt[0, 2048 * c:2048 * (c + 1)], x_sample[0, 1024 * c:1024 * (c + 1)],
                             x_sample[1, 1024 * c:1024 * (c + 1)]], axis=0)
        m = dict(shared)
        m['xT_in'] = np.ascontiguousarray(np.asarray(xt, f32).T)
        m.update(_core_tables(c))
        in_maps.append(m)
    res = run_bass_kernel_spmd(nc, in_maps, core_ids=list(range(NCORE)))
    yp = np.zeros((1, 16384, D), f32)
    ys = np.zeros((2, 8192, D), f32)
    for c in range(NCORE):
        y = np.asarray(res.results[c]["yT_out"], f32).T
        yp[0, 2048 * c:2048 * (c + 1)] = y[0:2048]
        ys[0, 1024 * c:1024 * (c + 1)] = y[2048:3072]
        ys[1, 1024 * c:1024 * (c + 1)] = y[3072:4096]
    return (yp, ys)
```

```python
import os
import numpy as np
import ml_dtypes
from contextlib import ExitStack
import concourse.bass as bass
import concourse.mybir as mybir
from concourse.bass_utils import run_bass_kernel_spmd

F32 = mybir.dt.float32
BF16 = mybir.dt.bfloat16
AF = mybir.ActivationFunctionType
ALU = mybir.AluOpType
NPBF = ml_dtypes.bfloat16

NCORE = 8
D = 1024
NT = 4096
DFF = 2816
EPS = 1e-6
DBG = int(os.environ.get('KDBG', '9'))
RUNKW = {}
LAST = {}
PHASES = [('attn', 0), ('ffn', 0), ('four', 0), ('ffn', 1), ('attn', 1), ('ffn', 2), ('four', 1), ('ffn', 3)]
SEGS = [(0, 2048, 16384), (2048, 1024, 8192), (3072, 1024, 8192)]


class Prog:
    def __init__(self):
        self.ops = []
        self.lastw = {}
        self.readers = {}
        self.stream_last = {}
        self.bar = None

    def barrier(self, fn):
        i = len(self.ops)
        lo = 0 if self.bar is None else self.bar
        deps = set(range(lo, i))
        if 'bar' in self.stream_last:
            deps.add(self.stream_last['bar'])
        self.ops.append(dict(eng='sp', fn=fn, deps=deps, stream='bar', inc=16))
        self.stream_last['bar'] = i
        self.bar = i
        return i

    def op(self, eng, fn, reads=(), writes=(), stream=None, inc=16):
        i = len(self.ops)
        deps = set()
        if self.bar is not None:
            deps.add(self.bar)
        for k in list(reads) + list(writes):
            if k in self.lastw:
                deps.add(self.lastw[k])
        for k in writes:
            deps.update(self.readers.get(k, ()))
        if stream is not None and stream in self.stream_last:
            deps.add(self.stream_last[stream])
        self.ops.append(dict(eng=eng, fn=fn, deps=deps, stream=stream, inc=inc))
        for k in writes:
            self.lastw[k] = i
            self.readers[k] = []
        for k in reads:
            self.readers.setdefault(k, []).append(i)
        if stream is not None:
            self.stream_last[stream] = i
        return i

    def emit(self, nc, es):
        ops = self.ops
        engs = ['pe', 'act', 'dve', 'pool', 'sp']
        esem = {e: es.enter_context(nc.semaphore("sem_" + e)) for e in engs}
        ssem = {}
        for o in ops:
            if o['stream'] is not None and o['stream'] not in ssem:
                ssem[o['stream']] = es.enter_context(nc.semaphore("st_" + o['stream']))
        needed = [False] * len(ops)
        for i, o in enumerate(ops):
            for d in o['deps']:
                if ops[d]['stream'] is None and ops[d]['eng'] == 'pe' and o['eng'] == 'pe' and o['stream'] is None:
                    continue
                needed[d] = True
        cnt = {e: 0 for e in engs}
        scnt = {s: 0 for s in ssem}
        sig = [None] * len(ops)
        for i, o in enumerate(ops):
            if o['stream'] is not None:
                scnt[o['stream']] += o['inc']
                sig[i] = (ssem[o['stream']], scnt[o['stream']])
            elif needed[i]:
                cnt[o['eng']] += 1
                sig[i] = (esem[o['eng']], cnt[o['eng']])
        per = {e: [i for i, o in enumerate(ops) if o['eng'] == e] for e in engs}
        with nc.Block() as block:
            def run(eng_name, eng):
                waited = {}
                for i in per[eng_name]:
                    o = ops[i]
                    need = {}
                    for d in o['deps']:
                        if sig[d] is None:
                            continue
                        if ops[d]['stream'] is None and ops[d]['eng'] == 'pe' and eng_name == 'pe' and o['stream'] is None:
                            continue
                        s, v = sig[d]
                        key = id(s)
                        if waited.get(key, 0) >= v:
                            continue
                        if key not in need or need[key][1] < v:
                            need[key] = (s, v)
                    for key, (s, v) in need.items():
                        eng.wait_ge(s, v)
                        waited[key] = v
                    ins = o['fn'](eng)
                    if sig[i] is not None and ins is not None:
                        if o['stream'] is not None:
                            if o['inc'] == 1:
                                ins.then_inc(sig[i][0])
                            else:
                                ins.then_inc(sig[i][0], o['inc'])
                        else:
                            ins.then_inc(sig[i][0], 1)

            @block.tensor
            def _(e):
                run('pe', e)

            @block.scalar
            def _(e):
                run('act', e)

            @block.vector
            def _(e):
                run('dve', e)

            @block.gpsimd
            def _(e):
                run('pool', e)

            @block.sync
            def _(e):
                run('sp', e)


def build_program():
    nc = bass.Bass("TRN2", target_bir_lowering=False)
    P = Prog()
    es = ExitStack()

    def din(name, shape, dt=F32):
        return nc.dram_tensor(name, list(shape), dt, kind="ExternalInput").ap()

    xT_in = din("xT_in", [D, NT])
    w_qkv = din("w_qkv", [2, D, 1536])
    w_o = din("w_o", [2, D, D])
    w_f = din("w_f", [2, D, D])
    w_gu = din("w_gu", [4, D, 2 * DFF])
    w_d = din("w_d", [4, DFF, D])
    gcols_d = din("gcols", [128, 64])
    qkg_d = din("qkg", [128, 4])
    sink_d = din("sinkrow", [1, 2 * 2048])
    sel_d = din("sel", [128, 16])
    cos_d = din("cosT", [128, NT])
    sin_d = din("sinT", [128, NT])
    cmat_d = din("cmat", [128, 4 * 128], BF16)
    masks_d = din("masks", [128, 8 * 512], BF16)
    f1_d = din("f1", [128, 2 * 256], BF16)
    g_d = din("gtab", [128, 2 * 128 * 64], BF16)
    ctab_d = din("ctab", [128, 2 * 8 * 128], BF16)
    yT_out = nc.dram_tensor("yT_out", [D, NT], F32, kind="ExternalOutput").ap()

    xT_d = nc.dram_tensor("xT_d", [D, NT], F32).ap()
    ag_sends = [nc.dram_tensor("ag_send%d" % i, [8 * 2048, 128], BF16).ap() for i in range(2)]
    ag_recvs = [nc.dram_tensor("ag_recv%d" % i, [8 * 8 * 2048, 128], BF16).ap() for i in range(2)]
    h_send = nc.dram_tensor("h_send", [4 * 128, 1024], BF16).ap()
    h_recv = nc.dram_tensor("h_recv", [8 * 4 * 128, 1024], BF16).ap()
    h_send2 = nc.dram_tensor("h_send2", [2 * 128, 1024], BF16).ap()
    h_recv2 = nc.dram_tensor("h_recv2", [8 * 2 * 128, 1024], BF16).ap()

    def sb(name, shape, dt):
        return es.enter_context(nc.sbuf_tensor(name, list(shape), dt))

    xb = sb("xb", [128, 8, 1024], F32)
    tmpA = sb("tmpA", [128, 512], F32)
    tmpB = sb("tmpB", [128, 512], F32)
    tmpC = sb("tmpC", [128, 512], F32)
    cosb = sb("cosb", [128, 512], F32)
    sinb = sb("sinb", [128, 512], F32)
    gcols = sb("gcols_s", [128, 64], F32)
    qkg = sb("qkg_s", [128, 4], F32)
    sel = sb("sel_s", [128, 16], F32)
    epsc = sb("epsc", [128, 1], F32)
    cmat = sb("cmat_s", [128, 512], BF16)
    ARENA = 83456
    arena = sb("arena", [128, ARENA], BF16)
    TT = [xb[:, k, 512:1024] for k in range(8)]
    ident = cmat[:, 0:128]
    ones = cmat[:, 128:256]
    onesblk = cmat[:, 256:384]
    Rm = cmat[:, 384:512]

    ps = [es.enter_context(nc.psum_tensor("ps%d" % i, [128, 512], F32)) for i in range(7)]
    psT = es.enter_context(nc.psum_tensor("psT", [128, 1024], BF16))

    class Carve:
        def __init__(self):
            self.off = 0

        def take(self, n):
            v = arena[:, self.off:self.off + n]
            self.off += n
            assert self.off <= ARENA, self.off
            return v

    def xview(t):
        return t.rearrange("(k p) t -> p k t", p=128)

    P.op('sp', lambda e: e.dma_start(out=gcols[:], in_=gcols_d), writes=['gcols'], stream='c0')
    P.op('sp', lambda e: e.dma_start(out=qkg[:], in_=qkg_d), writes=['qkg'], stream='c1')
    P.op('sp', lambda e: e.dma_start(out=sel[:], in_=sel_d), writes=['sel'], stream='c2')
    P.op('sp', lambda e: e.dma_start(out=cmat[:], in_=cmat_d), writes=['cmat'], stream='c3')
    P.op('dve', lambda e: e.memset(epsc[:], float(EPS)), writes=['epsc'])
    state = {'xsrc': xT_in}

    def barrier():
        P.barrier(lambda e: e.dma_start(out=h_send2[0:1, 0:16], in_=h_send2[1:2, 0:16]))

    def load_x(tb, col0=0):
        src = xview(state['xsrc'])[:, :, tb * 512:(tb + 1) * 512]
        P.op('sp', lambda e: e.dma_start(out=xb[:, :, col0:col0 + 512], in_=src),
             reads=[('x', tb)], writes=[('xb', col0)], stream='xb%d' % col0)

    def store_x(tb, col0=0, dst=None):
        d = xview(dst if dst is not None else xT_d)[:, :, tb * 512:(tb + 1) * 512]
        P.op('sp', lambda e: e.dma_start(out=d, in_=xb[:, :, col0:col0 + 512]),
             reads=[('xb', col0)], writes=[('x', tb)], stream='xs%d' % col0)

    def rmsnorm(gidx, xn, sqb, col0=0, xnkey='xn', sqkey='sqb'):
        xn3 = xn.rearrange("p (k t) -> p k t", k=8)
        sq3 = sqb.rearrange("p (k t) -> p k t", k=8)
        P.op('act', lambda e: e.activation(out=sq3, in_=xb[:, :, col0:col0 + 512], func=AF.Square),
             reads=[('xb', col0)], writes=[sqkey])
        for k in range(8):
            P.op('pe', lambda e, k=k: e.matmul(ps[0][:], ones, sq3[:, k, :], start=(k == 0), stop=(k == 7)),
                 reads=[sqkey, 'cmat'], writes=[('ps', 0)])
        P.op('act', lambda e: e.activation(out=tmpA[:], in_=ps[0][:], func=AF.Sqrt, bias=epsc[:, 0:1], scale=1.0 / D),
             reads=[('ps', 0), 'epsc'], writes=['tmpA'])
        P.op('dve', lambda e: e.reciprocal(out=tmpA[:], in_=tmpA[:]), reads=['tmpA'], writes=['tmpA'])
        for k in range(8):
            P.op('dve', lambda e, k=k: e.scalar_tensor_tensor(
                out=xn3[:, k, :], in0=xb[:, k, col0:col0 + 512], scalar=gcols[:, gidx * 8 + k:gidx * 8 + k + 1],
                in1=tmpA[:], op0=ALU.mult, op1=ALU.mult),
                reads=[('xb', col0), 'tmpA', 'gcols'], writes=[xnkey])

    def ffn(l, last):
        barrier()
        cv = Carve()
        xn = cv.take(8 * 1024)
        sqb = cv.take(8 * 512)
        act = cv.take(22 * 1024).rearrange("p (f t) -> p f t", f=22)
        wd = cv.take(22 * 1024).rearrange("p (f n) -> p f n", f=22)
        wgu = [cv.take(8 * 512).rearrange("p (k n) -> p k n", k=8) for _ in range(2)]
        xn3 = xn.rearrange("p (s k t) -> p s k t", s=2, k=8)
        wdsrc = w_d[l].rearrange("(f p) n -> p f n", p=128)
        for q in range(2):
            P.op('pool', lambda e, q=q: e.dma_start(out=wd[:, q * 11:(q + 1) * 11, :], in_=wdsrc[:, q * 11:(q + 1) * 11, :]),
                 writes=['wd'], stream='wd')
        gusrc = w_gu[l].rearrange("(k p) n -> p k n", p=128)
        for sbk in range(4):
            for s in range(2):
                tb = sbk * 2 + s
                load_x(tb, s * 512)
                rmsnorm(4 + l, xn[:, s * 4096:(s + 1) * 4096], sqb, s * 512, xnkey=('xn', s))
            for fp in range(11):
                slot = fp % 2
                w = wgu[slot]
                P.op('pool', lambda e, w=w, fp=fp: e.dma_start(out=w[:, :, 0:256], in_=gusrc[:, :, fp * 256:(fp + 1) * 256]),
                     writes=[('wgu', slot)], stream='wgu%d' % slot)
                P.op('pool', lambda e, w=w, fp=fp: e.dma_start(out=w[:, :, 256:512],
                                                               in_=gusrc[:, :, DFF + fp * 256:DFF + (fp + 1) * 256]),
                     writes=[('wgu', slot)], stream='wgu%d' % slot)
                for fc in range(2):
                    f = fp * 2 + fc
                    for s in range(2):
                        pg, pu = (1, 2) if (f * 2 + s) % 2 == 0 else (3, 4)
                        for k in range(8):
                            P.op('pe', lambda e, k=k, w=w, fc=fc, s=s, pg=pg: e.matmul(
                                ps[pg][:], w[:, k, fc * 128:(fc + 1) * 128], xn3[:, s, k, :], start=(k == 0), stop=(k == 7)),
                                reads=[('wgu', slot), ('xn', s)], writes=[('ps', pg)])
                        for k in range(8):
                            P.op('pe', lambda e, k=k, w=w, fc=fc, s=s, pu=pu: e.matmul(
                                ps[pu][:], w[:, k, 256 + fc * 128:256 + (fc + 1) * 128], xn3[:, s, k, :], start=(k == 0), stop=(k == 7)),
                                reads=[('wgu', slot), ('xn', s)], writes=[('ps', pu)])
                        tmp = tmpB if pg == 1 else tmpC
                        tk = 'tmpB' if pg == 1 else 'tmpC'
                        P.op('act', lambda e, pg=pg, tmp=tmp: e.activation(out=tmp[:], in_=ps[pg][:], func=AF.Silu),
                             reads=[('ps', pg)], writes=[tk])
                        P.op('dve', lambda e, pu=pu, tmp=tmp, f=f, s=s: e.tensor_tensor(
                            out=act[:, f, s * 512:(s + 1) * 512], in0=ps[pu][:], in1=tmp[:], op=ALU.mult),
                            reads=[('ps', pu), tk], writes=[('act', s)])
            for s in range(2):
                tb = sbk * 2 + s
                for dc in range(8):
                    pd = 5 + (dc % 2)
                    for f in range(22):
                        P.op('pe', lambda e, f=f, dc=dc, s=s, pd=pd: e.matmul(
                            ps[pd][:], wd[:, f, dc * 128:(dc + 1) * 128], act[:, f, s * 512:(s + 1) * 512],
                            start=(f == 0), stop=(f == 21)),
                            reads=['wd', ('act', s)], writes=[('ps', pd)])
                    P.op('dve', lambda e, dc=dc, s=s, pd=pd: e.tensor_tensor(
                        out=xb[:, dc, s * 512:(s + 1) * 512], in0=ps[pd][:], in1=xb[:, dc, s * 512:(s + 1) * 512], op=ALU.add),
                        reads=[('ps', pd), ('xb', s * 512)], writes=[('xb', s * 512)])
                store_x(tb, s * 512, dst=(yT_out if last else None))
        state['xsrc'] = xT_d

    def attention(la):
        barrier()
        cv = Carve()
        xn = cv.take(8 * 512)
        sqb = cv.take(8 * 512)
        Wq = cv.take(8 * 1024).rearrange("p (k n) -> p k n", k=8)
        Wkd = cv.take(8 * 512).rearrange("p (k n) -> p k n", k=8)
        Wv = cv.take(8 * 256).rearrange("p (k n) -> p k n", k=8)
        Qh = cv.take(8 * 2048).rearrange("p (c t) -> p c t", c=8)
        Klo = cv.take(4 * 2048).rearrange("p (h t) -> p h t", h=4)
        Khi = cv.take(4 * 2048).rearrange("p (h t) -> p h t", h=4)
        Vh = cv.take(16 * 512).rearrange("p (b n) -> p b n", b=16)
        halo = cv.take(4 * 1536).rearrange("p (i n) -> p i n", i=4)
        cand = cv.take(8 * 512).rearrange("p (j n) -> p j n", j=8)
        PT = cv.take(3 * 512).rearrange("p (j n) -> p j n", j=3)
        OT = xn.rearrange("p (c t) -> p c t", c=8)
        masks = cv.take(8 * 512).rearrange("p (m n) -> p m n", m=8)
        esink = cv.take(2048)
        qbfs = [cv.take(512), cv.take(512)]
        sqcs = [cv.take(512), cv.take(512)]
        qbf = qbfs[0]
        PTs = [PT, cand.rearrange("p j n -> p (j n)")[:, 0:1536].rearrange("p (j n) -> p j n", j=3)]
        cnt = {'qk': 0, 'hd': 0}
        xn3 = xn.rearrange("p (k t) -> p k t", k=8)
        P.op('sp', lambda e: e.dma_start(out=masks, in_=masks_d.rearrange("p (m n) -> p m n", m=8)), writes=['masks'], stream='c0')
        P.op('sp', lambda e: e.dma_start(out=tmpB[0:1, :], in_=sink_d[0:1, la * 2048:la * 2048 + 512]), writes=['tmpB'], stream='c1')
        for h in range(4):
            if h > 0:
                P.op('sp', lambda e, h=h: e.dma_start(out=tmpB[0:1, :], in_=sink_d[0:1, la * 2048 + h * 512:la * 2048 + (h + 1) * 512]),
                     writes=['tmpB'], stream='c1')
            P.op('act', lambda e, h=h: e.activation(out=esink[0:1, h * 512:(h + 1) * 512], in_=tmpB[0:1, :], func=AF.Exp),
                 reads=['tmpB'], writes=['esink'])
        qsrc = w_qkv[la].rearrange("(k p) n -> p k n", p=128)
        KK = [('K', i) for i in range(4)]
        HK = [('halo', i) for i in range(4)]
        P.op('pool', lambda e: e.memset(Klo[64:128, :, :], 0.0), writes=KK)
        P.op('pool', lambda e: e.memset(Khi[0:64, :, :], 0.0), writes=KK)
        P.op('pool', lambda e: e.memset(halo[64:128, :, 0:512], 0.0), writes=HK)
        P.op('pool', lambda e: e.memset(halo[0:64, :, 512:1024], 0.0), writes=HK)

        def load_w1():
            P.op('pool', lambda e: e.dma_start(out=Wq, in_=qsrc[:, :, 0:1024]), writes=['Wq'], stream='wq')
            for u in range(2):
                for kc in range(8):
                    P.op('pool', lambda e, u=u, kc=kc: e.dma_start(
                        out=Wkd[:, kc, :].rearrange("p (h u d) -> p h u d", h=4, u=2)[:, :, u, :],
                        in_=qsrc[:, kc, 1024:1280].rearrange("p (h d) -> p h d", h=4)), writes=['Wkd'], stream='wk')
            P.op('pool', lambda e: e.dma_start(out=Wv, in_=qsrc[:, :, 1280:1536]), writes=['Wv'], stream='wv')

        def qk_post(pp, gcol, dst, dkey):
            it = cnt['qk']
            cnt['qk'] += 1
            b = it % 2
            sqc, qb = sqcs[b], qbfs[b]
            pss, psr = (2, 3) if b == 0 else (5, 6)
            tA, tB, tC = TT[b], TT[2 + b], TT[4 + b]
            kA, kB, kC, kq, ks = ('T', b), ('T', 2 + b), ('T', 4 + b), ('qbf', b), ('sqc', b)
            P.op('act', lambda e: e.activation(out=sqc, in_=ps[pp][:], func=AF.Square), reads=[('ps', pp)], writes=[ks])
            P.op('pe', lambda e: e.matmul(ps[pss][:], onesblk, sqc, start=True, stop=True), reads=[ks, 'cmat'], writes=[('ps', pss)])
            P.op('act', lambda e: e.activation(out=tA, in_=ps[pss][:], func=AF.Sqrt, bias=epsc[:, 0:1], scale=1.0 / 64),
                 reads=[('ps', pss), 'epsc'], writes=[kA])
            P.op('dve', lambda e: e.reciprocal(out=tA, in_=tA), reads=[kA], writes=[kA])
            P.op('dve', lambda e: e.scalar_tensor_tensor(out=qb, in0=ps[pp][:], scalar=qkg[:, gcol:gcol + 1], in1=tA,
                                                         op0=ALU.mult, op1=ALU.mult),
                 reads=[('ps', pp), kA, 'qkg'], writes=[kq])
            P.op('pe', lambda e: e.matmul(ps[psr][:], Rm, qb, start=True, stop=True), reads=[kq, 'cmat'], writes=[('ps', psr)])
            P.op('pool', lambda e: e.tensor_tensor(out=tB, in0=qb, in1=cosb[:], op=ALU.mult),
                 reads=[kq, 'cos'], writes=[kB])
            P.op('dve', lambda e: e.tensor_tensor(out=tC, in0=ps[psr][:], in1=sinb[:], op=ALU.mult),
                 reads=[('ps', psr), 'sin'], writes=[kC])
            if isinstance(dst, tuple):
                dlo, dhi = dst
                P.op('pool', lambda e: e.tensor_tensor(out=dlo, in0=tB[0:64, :], in1=tC[0:64, :], op=ALU.add),
                     reads=[kB, kC], writes=[dkey])
                P.op('pool', lambda e: e.tensor_tensor(out=dhi, in0=tB[64:128, :], in1=tC[64:128, :], op=ALU.add),
                     reads=[kB, kC], writes=[dkey])
            else:
                P.op('pool', lambda e: e.tensor_tensor(out=dst, in0=tB, in1=tC, op=ALU.add),
                     reads=[kB, kC], writes=[dkey])

        for hf in range(2):
            segs = [(0, 16, 0)] if hf == 0 else [(0, 8, 1), (8, 8, 2)]
            load_w1()
            for tbl in range(4):
                tb = hf * 4 + tbl
                load_x(tb, 0)
                rmsnorm(la, xn, sqb, 0)
                P.op('sp', lambda e, tb=tb: e.dma_start(out=cosb[:], in_=cos_d[:, tb * 512:(tb + 1) * 512]), writes=['cos'], stream='cos')
                P.op('sp', lambda e, tb=tb: e.dma_start(out=sinb[:], in_=sin_d[:, tb * 512:(tb + 1) * 512]), writes=['sin'], stream='sin')
                for pc in range(12):
                    pp = 1 if pc % 2 == 0 else 4
                    for k in range(8):
                        if pc < 8:
                            P.op('pe', lambda e, k=k, pc=pc, pp=pp: e.matmul(ps[pp][:], Wq[:, k, pc * 128:(pc + 1) * 128], xn3[:, k, :],
                                                                            start=(k == 0), stop=(k == 7)),
                                 reads=['Wq', 'xn'], writes=[('ps', pp)])
                        else:
                            h = pc - 8
                            P.op('pe', lambda e, k=k, h=h, pp=pp: e.matmul(ps[pp][:], Wkd[:, k, h * 128:(h + 1) * 128], xn3[:, k, :],
                                                                          start=(k == 0), stop=(k == 7)),
                                 reads=['Wkd', 'xn'], writes=[('ps', pp)])
                    if pc < 8:
                        qk_post(pp, 2 * la, Qh[:, pc, tbl * 512:(tbl + 1) * 512], ('Q', tbl))
                    else:
                        qk_post(pp, 2 * la + 1, (Klo[0:64, pc - 8, tbl * 512:(tbl + 1) * 512],
                                                 Khi[64:128, pc - 8, tbl * 512:(tbl + 1) * 512]), ('K', tbl))
                for sub in range(4):
                    pv = 0
                    blk = tbl * 4 + sub
                    for k in range(8):
                        P.op('pe', lambda e, k=k, sub=sub, pv=pv: e.matmul(ps[pv][:, 0:256], xn3[:, k, sub * 128:(sub + 1) * 128], Wv[:, k, :],
                                                                          start=(k == 0), stop=(k == 7)),
                             reads=['Wv', 'xn'], writes=[('ps', pv)])
                    for u in range(2):
                        P.op('act', lambda e, u=u, blk=blk, pv=pv: e.activation(
                            out=Vh[:, blk, :].rearrange("p (h u d) -> p h u d", h=4, u=2)[:, :, u, :],
                            in_=ps[pv][:, 0:256].rearrange("p (h d) -> p h d", h=4), func=AF.Copy),
                            reads=[('ps', pv)], writes=[('V', tbl)])
            if DBG < 2:
                continue
            nh = 2 * len(segs)
            hs = h_send2 if hf == 0 else h_send
            hr = h_recv2 if hf == 0 else h_recv
            for si, (b0, nb, gs) in enumerate(segs):
                for side, blk in ((0, b0), (1, b0 + nb - 1)):
                    idx = si * 2 + side
                    P.op('sp', lambda e, idx=idx, blk=blk, hs=hs: e.dma_start(
                        out=hs[idx * 128:idx * 128 + 64, 0:512].rearrange("p (h t) -> p h t", h=4),
                        in_=Klo[0:64, :, blk * 128:(blk + 1) * 128]),
                        reads=[('K', blk // 4)], writes=['hsend'], stream='hs0')
                    P.op('sp', lambda e, idx=idx, blk=blk, hs=hs: e.dma_start(
                        out=hs[idx * 128 + 64:(idx + 1) * 128, 0:512].rearrange("p (h t) -> p h t", h=4),
                        in_=Khi[64:128, :, blk * 128:(blk + 1) * 128]),
                        reads=[('K', blk // 4)], writes=['hsend'], stream='hs0')
                    P.op('sp', lambda e, idx=idx, blk=blk, hs=hs: e.dma_start(
                        out=hs[idx * 128:(idx + 1) * 128, 512:1024], in_=Vh[:, blk, :]),
                        reads=[('V', blk // 4)], writes=['hsend'], stream='hs1')
            P.op('pool', lambda e, hs=hs, hr=hr: e.collective_compute(
                "AllGather", ALU.bypass, replica_groups=[list(range(NCORE))], ins=[hs.opt()], outs=[hr.opt()]),
                reads=['hsend'], writes=['hrecv'], stream='cc', inc=1)
            hr4 = hr.rearrange("(j i p) n -> p j i n", j=8, i=nh)
            for si, (b0, nb, gs) in enumerate(segs):
                for side in range(2):
                    hidx = si * 2 + side
                    srcidx = si * 2 + (1 - side)
                    for part in range(2):
                        P.op('sp', lambda e, srcidx=srcidx, hr4=hr4, part=part: e.dma_start(
                            out=cand, in_=hr4[:, :, srcidx, part * 512:(part + 1) * 512]),
                            reads=['hrecv'], writes=['cand'], stream='cand')
                        acc = qbf if part == 0 else halo[:, hidx, 1024:1536]
                        akey = ('qbf', 0) if part == 0 else ('halo', hidx)
                        for j in range(8):
                            sc = sel[:, side * 8 + j:side * 8 + j + 1]
                            if j == 0:
                                P.op('dve', lambda e, sc=sc, acc=acc: e.tensor_scalar(out=acc, in0=cand[:, 0, :], scalar1=sc,
                                                                                       scalar2=None, op0=ALU.mult),
                                     reads=['cand', 'sel'], writes=[akey])
                            else:
                                P.op('dve', lambda e, sc=sc, acc=acc, j=j: e.scalar_tensor_tensor(
                                    out=acc, in0=cand[:, j, :], scalar=sc, in1=acc, op0=ALU.mult, op1=ALU.add),
                                    reads=['cand', 'sel', akey], writes=[akey])
                        if part == 0:
                            P.op('dve', lambda e, hidx=hidx: e.tensor_copy(out=halo[0:64, hidx, 0:512], in_=qbf[0:64, :]),
                                 reads=[('qbf', 0)], writes=[('halo', hidx)])
                            P.op('dve', lambda e, hidx=hidx: e.tensor_copy(out=halo[64:128, hidx, 512:1024], in_=qbf[64:128, :]),
                                 reads=[('qbf', 0)], writes=[('halo', hidx)])
            if DBG < 3:
                continue
            barrier()
            P.op('pool', lambda e: e.dma_start(out=Wq, in_=w_o[la].rearrange("(k p) n -> p k n", p=128)), writes=['Wq'], stream='wq')
            for tbl in range(4):
                tb = hf * 4 + tbl
                load_x(tb, 0)
                for qb in range(4):
                    n = tbl * 4 + qb
                    si = [i for i, (b0, nb, gs) in enumerate(segs) if b0 <= n < b0 + nb][0]
                    b0, nb, gs = segs[si]
                    for h in range(4):
                        itn = cnt['hd']
                        cnt['hd'] += 1
                        bb = itn % 2
                        PTc = PTs[bb]
                        pO, pD = (4, 5) if bb == 0 else (0, 6)
                        tR = TT[6 + bb]
                        kR = ('T', 6 + bb)
                        for jj, j in enumerate((n - 1, n, n + 1)):
                            if b0 <= j < b0 + nb:
                                Ks = (Klo[:, h, j * 128:(j + 1) * 128], Khi[:, h, j * 128:(j + 1) * 128])
                                Vs = Vh[:, j, h * 128:(h + 1) * 128]
                                rk = [('K', j // 4), ('V', j // 4)]
                                m = None if j == n else (0 if j < n else 1)
                            else:
                                side = 0 if j < n else 1
                                hidx = si * 2 + side
                                Ks = (halo[:, hidx, h * 128:(h + 1) * 128], halo[:, hidx, 512 + h * 128:512 + (h + 1) * 128])
                                Vs = halo[:, hidx, 1024 + h * 128:1024 + (h + 1) * 128]
                                rk = [('halo', hidx)]
                                m = 2 + gs * 2 + side
                            pS = 1 + jj
                            kP = ('PT', bb, jj)
                            for half in range(2):
                                P.op('pe', lambda e, Ks=Ks, h=h, n=n, half=half, pS=pS: e.matmul(
                                    ps[pS][:, half * 256:(half + 1) * 256], Ks[half],
                                    Qh[:, 2 * h:2 * h + 2, n * 128:(n + 1) * 128], start=True, stop=True),
                                    reads=rk + [('Q', n // 4)], writes=[('ps', pS)])
                            P.op('act', lambda e, jj=jj, pS=pS, PTc=PTc: e.activation(out=PTc[:, jj, :], in_=ps[pS][:], func=AF.Exp, scale=0.125),
                                 reads=[('ps', pS)], writes=[kP])
                            if m is not None:
                                P.op('pool', lambda e, jj=jj, m=m, PTc=PTc: e.tensor_tensor(out=PTc[:, jj, :], in0=PTc[:, jj, :], in1=masks[:, m, :],
                                                                                   op=ALU.mult),
                                     reads=[kP, 'masks'], writes=[kP])
                            P.op('pe', lambda e, Vs=Vs, jj=jj, PTc=PTc, pO=pO: e.matmul(ps[pO][:], Vs, PTc[:, jj, :], start=(jj == 0), stop=(jj == 2)),
                                 reads=rk + [kP], writes=[('ps', pO)])
                            P.op('pe', lambda e, jj=jj, PTc=PTc, pD=pD: e.matmul(ps[pD][:], ones, PTc[:, jj, :], start=(jj == 0), stop=False),
                                 reads=['cmat', kP], writes=[('ps', pD)])
                        P.op('pe', lambda e, h=h, pD=pD: e.matmul(ps[pD][:], ones[0:1, :], esink[0:1, h * 512:(h + 1) * 512], start=False, stop=True),
                             reads=['cmat', 'esink'], writes=[('ps', pD)])
                        P.op('dve', lambda e, pD=pD, tR=tR: e.reciprocal(out=tR, in_=ps[pD][:]), reads=[('ps', pD)], writes=[kR])
                        for half in range(2):
                            lo = half * 64
                            P.op('dve', lambda e, lo=lo, half=half, h=h, qb=qb, pO=pO, tR=tR: e.tensor_tensor(
                                out=OT[lo:lo + 64, 2 * h:2 * h + 2, qb * 128:(qb + 1) * 128],
                                in0=ps[pO][lo:lo + 64, half * 256:(half + 1) * 256].rearrange("p (c t) -> p c t", c=2),
                                in1=tR[lo:lo + 64, half * 256:(half + 1) * 256].rearrange("p (c t) -> p c t", c=2), op=ALU.mult),
                                reads=[('ps', pO), kR], writes=['xn'])
                for dc in range(8):
                    pd = 1 if dc % 2 == 0 else 2
                    for pc in range(8):
                        P.op('pe', lambda e, pc=pc, dc=dc, pd=pd: e.matmul(ps[pd][:], Wq[:, pc, dc * 128:(dc + 1) * 128], OT[:, pc, :],
                                                                          start=(pc == 0), stop=(pc == 7)),
                             reads=['Wq', 'xn'], writes=[('ps', pd)])
                    P.op('dve', lambda e, dc=dc, pd=pd: e.tensor_tensor(out=xb[:, dc, 0:512], in0=ps[pd][:], in1=xb[:, dc, 0:512], op=ALU.add),
                         reads=[('ps', pd), ('xb', 0)], writes=[('xb', 0)])
                store_x(tb, 0)
            barrier()
        state['xsrc'] = xT_d

    def fourier(lf):
        barrier()
        cv = Carve()
        Xag = cv.take(128 * 128).rearrange("p (b c) -> p b c", b=128)
        U = cv.take(128 * 256)
        U3 = U.rearrange("p (c n) -> p c n", c=128)
        Wf = U[:, 0:8192].rearrange("p (k n) -> p k n", k=8)
        Vt = cv.take(2 * 2 * 2048).rearrange("p (a r t) -> p a r t", a=2, r=2)
        fT = cv.take(8 * 2048).rearrange("p (c t) -> p c t", c=8)
        G = cv.take(64 * 64).rearrange("p (k n) -> p k n", k=64)
        F1 = cv.take(512).rearrange("p (s n) -> p s n", s=2)
        Ct = cv.take(2048).rearrange("p (s i n) -> p s i n", s=2, i=8)
        xnb = [U[:, 8192:12288], U[:, 12288:16384]]
        sqbb = [U[:, 16384:20480], U[:, 20480:24576]]
        xtokb = [U[:, 24576:25600], U[:, 25600:26624]]
        P.op('sp', lambda e: e.dma_start(out=F1, in_=f1_d.rearrange("p (s n) -> p s n", s=2)), writes=['F1'], stream='c0')
        P.op('sp', lambda e: e.dma_start(out=Ct, in_=ctab_d.rearrange("p (s i n) -> p s i n", s=2, i=8)), writes=['Ct'], stream='c1')
        send3 = [a.rearrange("(c t) k -> t c k", c=8) for a in ag_sends]
        for tb in range(8):
            bq = tb % 2
            col0 = bq * 512
            hset = tb // 4
            load_x(tb, col0)
            rmsnorm(2 + lf, xnb[bq], sqbb[bq], col0, xnkey=('xn', bq), sqkey=('sqb', bq))
            xn3 = xnb[bq].rearrange("p (k t) -> p k t", k=8)
            for sub in range(4):
                tq_ = (tb * 4 + sub) % 2
                xtok = xtokb[tq_]
                for k in range(8):
                    P.op('pe', lambda e, k=k, sub=sub, xn3=xn3: e.transpose(psT[:, k * 128:(k + 1) * 128], xn3[:, k, sub * 128:(sub + 1) * 128], ident),
                         reads=[('xn', bq), 'cmat'], writes=['psT'])
                P.op('act', lambda e, xtok=xtok: e.activation(out=xtok, in_=psT[:], func=AF.Copy), reads=['psT'], writes=[('xtok', tq_)])
                t0 = (tb % 4) * 512 + sub * 128
                P.op('sp', lambda e, t0=t0, xtok=xtok, hset=hset: e.dma_start(out=send3[hset][t0:t0 + 128, :, :],
                                                                            in_=xtok.rearrange("p (c k) -> p c k", c=8)),
                     reads=[('xtok', tq_)], writes=[('agsend', hset)], stream='ags%d' % tq_)
            if tb % 4 == 3:
                P.op('pool', lambda e, hset=hset: e.collective_compute("AllGather", ALU.bypass, replica_groups=[list(range(NCORE))],
                                                                      ins=[ag_sends[hset].opt()], outs=[ag_recvs[hset].opt()]),
                     reads=[('agsend', hset)], writes=[('agrecv', hset)], stream='cc', inc=1)
        recv4 = [a.rearrange("(j c t) k -> j c t k", j=8, c=8) for a in ag_recvs]
        wfsrc = w_f[lf].rearrange("(k p) n -> p k n", p=128)
        for st in range(2):
            for g in range(4):
                for cl in range(2):
                    cc = 2 * g + cl
                    nd = 0
                    for j in range(8):
                        if st == 0:
                            P.op('sp', lambda e, j=j, cc=cc: e.dma_start(
                                out=Xag[16 * j:16 * j + 16, :, :],
                                in_=recv4[0][j, cc, 0:2048, :].rearrange("(a b) k -> a b k", a=16)),
                                reads=[('agrecv', 0)], writes=['Xag'], stream='xag%d' % (nd % 4))
                            nd += 1
                        else:
                            for s in range(2):
                                P.op('sp', lambda e, j=j, cc=cc, s=s: e.dma_start(
                                    out=Xag[64 * s + 8 * j:64 * s + 8 * j + 8, :, :],
                                    in_=recv4[1][j, cc, 1024 * s:1024 + 1024 * s, :].rearrange("(a b) k -> a b k", a=8)),
                                    reads=[('agrecv', 1)], writes=['Xag'], stream='xag%d' % (nd % 4))
                                nd += 1
                    for c2 in range(64):
                        pu = (1, 2, 5, 6)[c2 % 4]
                        for u in range(2):
                            c = c2 * 2 + u
                            P.op('pe', lambda e, c=c, u=u, pu=pu, st=st: e.matmul(ps[pu][:, u * 256:(u + 1) * 256], Xag[:, :, c], F1[:, st, :],
                                                                                  start=True, stop=True),
                                 reads=['Xag', 'F1'], writes=[('ps', pu)])
                        eng = 'act' if c2 % 2 == 0 else 'dve'
                        if eng == 'act':
                            P.op('act', lambda e, c2=c2, pu=pu: e.activation(out=U[:, c2 * 512:(c2 + 1) * 512], in_=ps[pu][:], func=AF.Copy),
                                 reads=[('ps', pu)], writes=['U'])
                        else:
                            P.op('dve', lambda e, c2=c2, pu=pu: e.tensor_copy(out=U[:, c2 * 512:(c2 + 1) * 512], in_=ps[pu][:]),
                                 reads=[('ps', pu)], writes=['U'])
                    for k16 in range(8):
                        pv = 3 + k16 % 2
                        if k16 % 4 == 0:
                            gh = k16 // 4
                            P.op('pool', lambda e, st=st, gh=gh: e.dma_start(
                                out=G, in_=g_d[:, st * 8192 + gh * 4096:st * 8192 + (gh + 1) * 4096].rearrange("p (k n) -> p k n", k=64)),
                                writes=['G'], stream='gld')
                        for kq in range(16):
                            kk = k16 * 16 + kq
                            P.op('pe', lambda e, kk=kk, kq=kq, pv=pv: e.matmul(ps[pv][:, kq * 32:(kq + 1) * 32], U3[:, :, kk], G[:, kk % 64, 0:32],
                                                                              start=True, stop=False),
                                 reads=['U', 'G'], writes=[('ps', pv)])
                            P.op('pe', lambda e, kk=kk, kq=kq, pv=pv: e.matmul(ps[pv][:, kq * 32:(kq + 1) * 32], U3[:, :, 128 + kk], G[:, kk % 64, 32:64],
                                                                              start=False, stop=True),
                                 reads=['U', 'G'], writes=[('ps', pv)])
                        for r in range(2):
                            src = ps[pv][:].rearrange("p (k r l) -> p r l k", k=16, r=2, l=16)[:, r, :, :]
                            if st == 0:
                                dst = Vt[:, cl, r, :].rearrange("p (l k) -> p l k", l=16)[:, :, k16 * 16:(k16 + 1) * 16]
                            else:
                                sq_, k1_0 = divmod(k16 * 16, 64)
                                dst = Vt[:, cl, r, sq_ * 1024:(sq_ + 1) * 1024].rearrange("p (l k) -> p l k", l=16)[:, :, k1_0:k1_0 + 16]
                            if r == 0:
                                P.op('act', lambda e, src=src, dst=dst: e.activation(out=dst, in_=src, func=AF.Copy),
                                     reads=[('ps', pv)], writes=[('Vt', cl)])
                            else:
                                P.op('dve', lambda e, src=src, dst=dst: e.tensor_copy(out=dst, in_=src),
                                     reads=[('ps', pv)], writes=[('Vt', cl)])
                for c2l in range(2):
                    for tq in range(4):
                        first = True
                        for cl in range(2):
                            for r in range(2):
                                i = (cl * 2 + r) * 2 + c2l
                                last_ = (cl == 1 and r == 1)
                                P.op('pe', lambda e, i=i, cl=cl, r=r, tq=tq, st=st, first=first, last_=last_: e.matmul(
                                    ps[0][:], Ct[:, st, i, :], Vt[:, cl, r, tq * 512:(tq + 1) * 512], start=first, stop=last_),
                                    reads=['Ct', ('Vt', cl)], writes=[('ps', 0)])
                                first = False
                        P.op('act', lambda e, g=g, c2l=c2l, tq=tq: e.activation(out=fT[:, 2 * g + c2l, tq * 512:(tq + 1) * 512], in_=ps[0][:], func=AF.Copy),
                             reads=[('ps', 0)], writes=['fT'])
            P.op('pool', lambda e: e.dma_start(out=Wf, in_=wfsrc), writes=['U'], stream='wq')
            for tq in range(4):
                tb = st * 4 + tq
                oc = (tq % 2) * 512
                load_x(tb, oc)
                for dc in range(8):
                    pd = 6 if dc % 2 == 0 else 5
                    for c in range(8):
                        P.op('pe', lambda e, c=c, dc=dc, pd=pd, tq=tq: e.matmul(ps[pd][:], Wf[:, c, dc * 128:(dc + 1) * 128], fT[:, c, tq * 512:(tq + 1) * 512],
                                                                              start=(c == 0), stop=(c == 7)),
                             reads=['U', 'fT'], writes=[('ps', pd)])
                    P.op('dve', lambda e, dc=dc, pd=pd, oc=oc: e.tensor_tensor(out=xb[:, dc, oc:oc + 512], in0=ps[pd][:], in1=xb[:, dc, oc:oc + 512], op=ALU.add),
                         reads=[('ps', pd), ('xb', oc)], writes=[('xb', oc)])
                store_x(tb, oc)
        state['xsrc'] = xT_d

    last_direct = False
    for pi, (kind, idx) in enumerate(PHASES):
        if kind == 'attn':
            attention(idx)
        elif kind == 'four':
            fourier(idx)
        else:
            last_direct = (pi == len(PHASES) - 1)
            ffn(idx, last=last_direct)
    barrier()
    if not last_direct:
        for tb in range(8):
            load_x(tb, 0)
            store_x(tb, 0, dst=yT_out)
        barrier()
    P.op('sp', lambda e: e.dma_start(out=h_send2[2:3, 0:16], in_=h_send2[3:4, 0:16]), stream='fin')
    barrier()
    with es:
        P.emit(nc, es)
    return nc


def _tables():
    c = {}
    ident = np.eye(128, dtype=np.float32)
    ones = np.ones((128, 128), np.float32)
    onesblk = np.zeros((128, 128), np.float32)
    onesblk[:64, :64] = 1
    onesblk[64:, 64:] = 1
    Rm = np.zeros((128, 128), np.float32)
    for hh in range(2):
        for d in range(32):
            Rm[hh * 64 + d + 32, hh * 64 + d] = -1.0
            Rm[hh * 64 + d, hh * 64 + d + 32] = 1.0
    c['cmat'] = np.concatenate([ident, ones, onesblk, Rm], axis=1).astype(NPBF)
    cc = np.arange(256)
    ang = 2 * np.pi * np.outer(cc, cc) / 256.0
    Cc, Sc = np.cos(ang), np.sin(ang)
    ct = np.zeros((128, 2, 8, 128), np.float64)
    for st, S in enumerate((16384, 8192)):
        sc = 1.0 / np.sqrt(S * 256.0)
        for cl in range(2):
            for r in range(2):
                for c2 in range(2):
                    M = Cc if r == 0 else Sc
                    ct[:, st, (cl * 2 + r) * 2 + c2, :] = sc * M[cl * 128:(cl + 1) * 128, c2 * 128:(c2 + 1) * 128]
    c['ctab'] = ct.reshape(128, -1).astype(NPBF)
    f1 = np.zeros((128, 2, 2, 128), np.float64)
    a = np.arange(128)
    ang = 2 * np.pi * np.outer(a, a) / 128.0
    f1[:, 0, 0, :] = np.cos(ang)
    f1[:, 0, 1, :] = -np.sin(ang)
    a64 = np.arange(64)
    ang = 2 * np.pi * np.outer(a64, a64) / 64.0
    for s in range(2):
        f1[s * 64:(s + 1) * 64, 1, 0, s * 64:(s + 1) * 64] = np.cos(ang)
        f1[s * 64:(s + 1) * 64, 1, 1, s * 64:(s + 1) * 64] = -np.sin(ang)
    c['f1'] = f1.reshape(128, -1).astype(NPBF)
    return c


def _core_tables(c):
    t = {}
    b = np.arange(128)[:, None, None]
    l = np.arange(16)[None, None, :]
    g = np.zeros((128, 2, 128, 64), np.float64)
    k1 = np.arange(128)[None, :, None]
    ang = 2 * np.pi * ((b * (k1 + 128 * (16 * c + l))) % 16384) / 16384.0
    Gr, Gi = np.cos(ang), -np.sin(ang)
    g[:, 0, :, 0:16], g[:, 0, :, 16:32], g[:, 0, :, 32:48], g[:, 0, :, 48:64] = Gr, Gi, -Gi, Gr
    k1 = (np.arange(128) % 64)[None, :, None]
    ang = 2 * np.pi * ((b * (k1 + 64 * (16 * c + l))) % 8192) / 8192.0
    Gr, Gi = np.cos(ang), -np.sin(ang)
    g[:, 1, :, 0:16], g[:, 1, :, 16:32], g[:, 1, :, 32:48], g[:, 1, :, 48:64] = Gr, Gi, -Gi, Gr
    t['gtab'] = g.reshape(128, -1).astype(NPBF)
    pos = np.concatenate([2048 * c + np.arange(2048), 1024 * c + np.arange(1024), 1024 * c + np.arange(1024)]).astype(np.float32)
    half = 32
    inv = (np.float32(10000.0) ** (-np.arange(half, dtype=np.float32) / half)).astype(np.float32)
    angf = pos[None, :] * inv[:, None]
    d = np.arange(128) % 32
    t['cosT'] = np.cos(angf)[d].astype(np.float32)
    t['sinT'] = np.sin(angf)[d].astype(np.float32)
    kk = np.arange(128)[:, None]
    qq = np.arange(128)[None, :]
    tri_ge = (kk >= qq).astype(np.float32)
    tri_le = (kk <= qq).astype(np.float32)
    m = np.zeros((128, 8, 4, 128), np.float32)
    m[:, 0] = tri_ge[:, None, :]
    m[:, 1] = tri_le[:, None, :]
    for gs in range(3):
        if c > 0:
            m[:, 2 + gs * 2 + 0] = tri_ge[:, None, :]
        if c < 7:
            m[:, 2 + gs * 2 + 1] = tri_le[:, None, :]
    t['masks'] = m.reshape(128, -1).astype(NPBF)
    s = np.zeros((128, 16), np.float32)
    if c > 0:
        s[:, c - 1] = 1.0
    if c < 7:
        s[:, 8 + c + 1] = 1.0
    t['sel'] = s
    return t


_NC = None


def kernel(x_prompt, x_sample, attn_norm_g, w_qkv, q_norm_g, k_norm_g, attn_sinks, w_o_attn,
           fourier_norm_g, w_fourier_out, ffn_norm_g, w_gate_up, w_down):
    global _NC
    if _NC is None:
        _NC = build_program()
    nc = _NC
    f32 = np.float32
    common = _tables()
    gl = [attn_norm_g[0], attn_norm_g[1], fourier_norm_g[0], fourier_norm_g[1],
          ffn_norm_g[0], ffn_norm_g[1], ffn_norm_g[2], ffn_norm_g[3]]
    gcols = np.concatenate([np.asarray(g, f32).reshape(8, 128).T for g in gl], axis=1)
    qkg = np.stack([np.tile(np.asarray(q_norm_g[0], f32), 2), np.tile(np.asarray(k_norm_g[0], f32), 2),
                    np.tile(np.asarray(q_norm_g[1], f32), 2), np.tile(np.asarray(k_norm_g[1], f32), 2)], axis=1)
    sr = np.zeros((2, 4, 4, 128), f32)
    for la in range(2):
        for h in range(4):
            for ci, gq in enumerate((0, 2, 1, 3)):
                sr[la, h, ci, :] = attn_sinks[la][4 * h + gq]
    sinkrow = sr.reshape(1, -1)
    shared = dict(w_qkv=np.ascontiguousarray(w_qkv, f32), w_o=np.ascontiguousarray(w_o_attn, f32),
                  w_f=np.ascontiguousarray(w_fourier_out, f32), w_gu=np.ascontiguousarray(w_gate_up, f32),
                  w_d=np.ascontiguousarray(w_down, f32), gcols=np.ascontiguousarray(gcols), qkg=np.ascontiguousarray(qkg),
                  sinkrow=np.ascontiguousarray(sinkrow), cmat=common['cmat'], f1=common['f1'], ctab=common['ctab'])
    in_maps = []
    for c in range(NCORE):
        xt = np.concatenate([x_prompt[0, 2048 * c:2048 * (c + 1)], x_sample[0, 1024 * c:1024 * (c + 1)],
                             x_sample[1, 1024 * c:1024 * (c + 1)]], axis=0)
        m = dict(shared)
        m['xT_in'] = np.ascontiguousarray(np.asarray(xt, f32).T)
        m.update(_core_tables(c))
        in_maps.append(m)
    res = run_bass_kernel_spmd(nc, in_maps, core_ids=list(range(NCORE)), **RUNKW)
    LAST['res'] = res
    yp = np.zeros((1, 16384, D), f32)
    ys = np.zeros((2, 8192, D), f32)
    for c in range(NCORE):
        y = np.asarray(res.results[c]["yT_out"], f32).T
        yp[0, 2048 * c:2048 * (c + 1)] = y[0:2048]
        ys[0, 1024 * c:1024 * (c + 1)] = y[2048:3072]
        ys[1, 1024 * c:1024 * (c + 1)] = y[3072:4096]
    return (yp, ys)
```
